# Optimizing a Trainium2 kernel written in Bass

```python
import jax, jax.numpy as jnp
from jax import lax
import numpy as np

D_MODEL = 4096
BATCH = 8
SEQ = 2048
DEPTH = 2

HEAD_DIM = 128
N_HEADS_A = 16
N_HEADS_B = 16
DILATED_PATTERNS = ((128, 1), (512, 4), (2048, 16))
MOBA_BLOCK = 256
MOBA_TOPK = 3
MOBA_QCHUNK = 16
N_HEADS_C = 16
SB_QBLOCK = 128
N_HEADS_D = 8
DQK_D = 128
DV_D = 256
MLSTM_CHUNK = 64
CONV_D = 4
D_FF = 11008
FFN_CONV = 3
LN_EPS = 1e-5
ALPHA = (2 * DEPTH) ** 0.25
BETA = (8 * DEPTH) ** -0.25
N_EVEN = (DEPTH + 1) // 2
N_ODD = DEPTH // 2

W_A = N_HEADS_A * HEAD_DIM
W_B = N_HEADS_B * HEAD_DIM
W_C = N_HEADS_C * HEAD_DIM
QK_D = N_HEADS_D * DQK_D
W_D = N_HEADS_D * DV_D
IN_AB = 3 * (W_A + W_B)
IN_CD = 3 * W_C + 2 * QK_D + 2 * W_D + 2 * N_HEADS_D
MIX_AB = W_A + W_B
MIX_CD = W_C + W_D

kernel_name = "hybrid_dilated_moba_stickbreak_mlstm"


def _split(t, sizes):
    return jnp.split(t, [int(s) for s in np.cumsum(sizes)[:-1]], axis=-1)


def layer_norm(x, g, b):
    xf = x.astype(jnp.float32)
    mu = xf.mean(-1, keepdims=True)
    var = jnp.mean(jnp.square(xf - mu), -1, keepdims=True)
    return ((xf - mu) * lax.rsqrt(var + LN_EPS)).astype(x.dtype) * g + b


def causal_dwconv(x, w, b):
    K, S = w.shape[0], x.shape[1]
    xp = jnp.pad(x, ((0, 0), (K - 1, 0), (0, 0)))
    return sum(w[k] * xp[:, k:k + S] for k in range(K)) + b


def dilated_branch(q, k, v, window, dilation):
    Bsz, S, H, Dh = q.shape
    L = S // dilation
    nw = window // dilation
    blk = nw
    nb = -(-L // blk)
    Lp = nb * blk

    def strided(t):
        t = t.reshape(Bsz, L, dilation, H, Dh).transpose(0, 2, 3, 1, 4)
        return jnp.pad(t, ((0, 0), (0, 0), (0, 0), (0, Lp - L), (0, 0)))

    def band(t):
        tp = jnp.pad(t, ((0, 0), (0, 0), (0, 0), (blk, 0), (0, 0)))
        tp = tp.reshape(Bsz, dilation, H, nb + 1, blk, Dh)
        return jnp.concatenate([tp[:, :, :, :-1], tp[:, :, :, 1:]], axis=4)

    qb = strided(q).reshape(Bsz, dilation, H, nb, blk, Dh)
    kb, vb = band(strided(k)), band(strided(v))
    s = jnp.einsum('bdhnqe,bdhnke->bdhnqk', qb, kb).astype(jnp.float32) * (Dh ** -0.5)
    qi = jnp.arange(blk)[:, None]
    ki = jnp.arange(2 * blk)[None, :] - blk
    dist = qi - ki
    kpos = jnp.arange(nb)[:, None, None] * blk + ki[None]
    valid = (dist >= 0) & (dist <= nw) & (kpos >= 0)
    s = jnp.where(valid, s, -jnp.inf)
    mx = s.max(-1)
    p = jnp.exp(s - mx[..., None])
    den = p.sum(-1)
    num = jnp.einsum('bdhnqk,bdhnke->bdhnqe', p.astype(vb.dtype), vb)

    def unstrided(t):
        rest = t.shape[5:]
        t = t.reshape(Bsz, dilation, H, Lp, *rest)[:, :, :, :L]
        t = jnp.moveaxis(t, 3, 1)
        return t.reshape(Bsz, S, H, *rest)

    return unstrided(num), unstrided(den), unstrided(mx)


def dilated_attention(q, k, v):
    outs = [dilated_branch(q, k, v, w, d) for (w, d) in DILATED_PATTERNS]
    m = jnp.max(jnp.stack([o[2] for o in outs]), axis=0)
    num = sum(o[0].astype(jnp.float32) * jnp.exp(o[2] - m)[..., None] for o in outs)
    den = sum(o[1] * jnp.exp(o[2] - m) for o in outs)
    return (num / den[..., None]).astype(q.dtype)


def moba_attention(q, k, v):
    Bsz, S, H, Dh = q.shape
    nblk = -(-S // MOBA_BLOCK)
    Sp = nblk * MOBA_BLOCK
    topk = min(MOBA_TOPK, nblk)
    pad = lambda t: jnp.pad(t, ((0, 0), (0, Sp - S), (0, 0), (0, 0))).transpose(0, 2, 1, 3)
    qh = pad(q)
    kh = pad(k).reshape(Bsz, H, nblk, MOBA_BLOCK, Dh)
    vh = pad(v).reshape(Bsz, H, nblk, MOBA_BLOCK, Dh)
    kmean = kh.mean(axis=3)
    bidx = jnp.arange(Bsz)[:, None, None, None]
    hidx = jnp.arange(H)[None, :, None, None]
    scale = Dh ** -0.5

    def chunk(ci):
        t0 = ci * MOBA_QCHUNK
        qc = lax.dynamic_slice_in_dim(qh, t0, MOBA_QCHUNK, axis=2)
        own = t0 // MOBA_BLOCK
        gate = jnp.einsum('bhqe,bhne->bhqn', qc, kmean).astype(jnp.float32)
        gate = jnp.where(jnp.arange(nblk) < own, gate, -jnp.inf)
        _, idx = lax.top_k(gate, topk)
        sel_ok = idx < own
        ksel = kh[bidx, hidx, idx]
        vsel = vh[bidx, hidx, idx].reshape(Bsz, H, MOBA_QCHUNK, topk * MOBA_BLOCK, Dh)
        s_sel = jnp.einsum('bhqe,bhqjke->bhqjk', qc, ksel).reshape(Bsz, H, MOBA_QCHUNK, topk * MOBA_BLOCK)
        s_sel = jnp.where(jnp.repeat(sel_ok, MOBA_BLOCK, axis=-1), s_sel.astype(jnp.float32), -jnp.inf)
        ko = lax.dynamic_index_in_dim(kh, own, axis=2, keepdims=False)
        vo = lax.dynamic_index_in_dim(vh, own, axis=2, keepdims=False)
        s_own = jnp.einsum('bhqe,bhke->bhqk', qc, ko).astype(jnp.float32)
        qpos = t0 + jnp.arange(MOBA_QCHUNK)
        kpos = own * MOBA_BLOCK + jnp.arange(MOBA_BLOCK)
        s_own = jnp.where(kpos[None, :] <= qpos[:, None], s_own, -jnp.inf)
        p = jax.nn.softmax(jnp.concatenate([s_sel, s_own], axis=-1) * scale, axis=-1)
        p_sel, p_own = p[..., :topk * MOBA_BLOCK], p[..., topk * MOBA_BLOCK:]
        return (jnp.einsum('bhqk,bhqke->bhqe', p_sel.astype(v.dtype), vsel)
                + jnp.einsum('bhqk,bhke->bhqe', p_own.astype(v.dtype), vo))

    out = lax.map(chunk, jnp.arange(Sp // MOBA_QCHUNK))
    out = out.transpose(1, 0, 3, 2, 4).reshape(Bsz, Sp, H, Dh)[:, :S]
    return out.astype(q.dtype)


def stick_breaking_attention(q, k, v):
    Bsz, S, H, Dh = q.shape
    qh, kh, vh = (t.transpose(0, 2, 1, 3) for t in (q, k, v))
    kpos = jnp.arange(S)

    def block(bi):
        t0 = bi * SB_QBLOCK
        qc = lax.dynamic_slice_in_dim(qh, t0, SB_QBLOCK, axis=2)
        z = jnp.einsum('bhqe,bhke->bhqk', qc, kh).astype(jnp.float32) * (Dh ** -0.5)
        qpos = t0 + jnp.arange(SB_QBLOCK)
        causal = kpos[None, :] < qpos[:, None]
        sp = jnp.where(causal, jax.nn.softplus(z), 0.0)
        rc = lax.cumsum(sp, axis=3, reverse=True)
        A = jnp.exp(jnp.where(causal, z - rc, -jnp.inf))
        return jnp.einsum('bhqk,bhke->bhqe', A.astype(v.dtype), vh)

    out = lax.map(block, jnp.arange(S // SB_QBLOCK))
    return out.transpose(1, 0, 3, 2, 4).reshape(Bsz, S, H, Dh).astype(q.dtype)


def mlstm_chunkwise(q, k, v, i_pre, f_pre):
    f32 = jnp.float32
    Bsz, S, H, Dk = q.shape
    Dv = v.shape[-1]
    L = MLSTM_CHUNK
    nc = S // L
    q, v = q.astype(f32), v.astype(f32)
    k = k.astype(f32) * (Dk ** -0.5)
    log_f = jax.nn.log_sigmoid(f_pre.astype(f32))
    log_i = i_pre.astype(f32)

    def to_chunks(t):
        t = t.reshape(Bsz, nc, L, H, *t.shape[3:])
        return jnp.moveaxis(t, [1, 3], [0, 2])

    tri = jnp.tril(jnp.ones((L, L), dtype=bool))

    def step(carry, xs):
        C, n, m = carry
        qc, kc, vc, lic, lfc = xs
        b = jnp.cumsum(lfc, axis=-1)
        g = b[..., -1]
        Dm = jnp.where(tri, b[..., :, None] - b[..., None, :] + lic[..., None, :], -jnp.inf)
        inter = b + m[..., None]
        m_t = jnp.maximum(inter, Dm.max(-1))
        P = jnp.exp(Dm - m_t[..., None])
        w_inter = jnp.exp(inter - m_t)
        Sqk = jnp.einsum('bhte,bhse->bhts', qc, kc) * P
        num = (w_inter[..., None] * jnp.einsum('bhve,bhte->bhtv', C, qc)
               + jnp.einsum('bhts,bhsv->bhtv', Sqk, vc))
        den = w_inter * jnp.einsum('bhe,bhte->bht', n, qc) + Sqk.sum(-1)
        h = num / jnp.maximum(jnp.abs(den), jnp.exp(-m_t))[..., None]
        dec = g[..., None] - b + lic
        m_new = jnp.maximum(g + m, dec.max(-1))
        wk = jnp.exp(dec - m_new[..., None])
        sc = jnp.exp(g + m - m_new)
        C_new = sc[..., None, None] * C + jnp.einsum('bhs,bhsv,bhse->bhve', wk, vc, kc)
        n_new = sc[..., None] * n + jnp.einsum('bhs,bhse->bhe', wk, kc)
        return (C_new, n_new, m_new), h

    init = (jnp.zeros((Bsz, H, Dv, Dk), f32), jnp.zeros((Bsz, H, Dk), f32), jnp.zeros((Bsz, H), f32))
    xs = (to_chunks(q), to_chunks(k), to_chunks(v), to_chunks(log_i), to_chunks(log_f))
    _, h = lax.scan(step, init, xs)
    return jnp.moveaxis(h, [0, 2], [1, 3]).reshape(Bsz, S, H, Dv)


def mixer_ab(x, w_in, w_out):
    Bsz, S, _ = x.shape
    qa, ka, va, qb, kb, vb = _split(x @ w_in, [W_A, W_A, W_A, W_B, W_B, W_B])
    heads = lambda t: t.reshape(Bsz, S, -1, HEAD_DIM)
    ya = dilated_attention(heads(qa), heads(ka), heads(va)).reshape(Bsz, S, W_A)
    yb = moba_attention(heads(qb), heads(kb), heads(vb)).reshape(Bsz, S, W_B)
    return jnp.concatenate([ya, yb], axis=-1).astype(x.dtype) @ w_out


def mixer_cd(x, w_in, b_if, conv_w, conv_b, norm_g, w_out):
    Bsz, S, _ = x.shape
    qc, kc, vc, qkd, vd, od, gd = _split(x @ w_in, [W_C, W_C, W_C, 2 * QK_D, W_D, W_D, 2 * N_HEADS_D])
    hc = lambda t: t.reshape(Bsz, S, N_HEADS_C, HEAD_DIM)
    yc = stick_breaking_attention(hc(qc), hc(kc), hc(vc)).reshape(Bsz, S, W_C)
    qkd = jax.nn.silu(causal_dwconv(qkd, conv_w, conv_b))
    qd, kd = _split(qkd, [QK_D, QK_D])
    gates = gd + b_if
    h = mlstm_chunkwise(qd.reshape(Bsz, S, N_HEADS_D, DQK_D), kd.reshape(Bsz, S, N_HEADS_D, DQK_D),
                        vd.reshape(Bsz, S, N_HEADS_D, DV_D), gates[..., :N_HEADS_D], gates[..., N_HEADS_D:])
    mu = h.mean(-1, keepdims=True)
    var = jnp.mean(jnp.square(h - mu), -1, keepdims=True)
    h = ((h - mu) * lax.rsqrt(var + LN_EPS)).reshape(Bsz, S, W_D) * norm_g
    yd = jax.nn.sigmoid(od.astype(jnp.float32)) * h
    return jnp.concatenate([yc, yd.astype(x.dtype)], axis=-1).astype(x.dtype) @ w_out


def conv_ffn(x, w_up, w_gate, conv_w, conv_b, w_down):
    u = x @ w_up
    g = causal_dwconv(x @ w_gate, conv_w, conv_b)
    return (jax.nn.silu(g) * u) @ w_down


def setup_inputs(seed: int = 0) -> dict:
    key = jax.random.key(seed)
    ks = jax.random.split(key, 18)
    nrm = lambda k, shape, scale: jax.random.normal(k, shape, jnp.float32) * scale
    NE, NO = N_EVEN, N_ODD
    x = nrm(ks[0], (BATCH, SEQ, D_MODEL), 1.0)
    w_in_ab = nrm(ks[1], (NE, D_MODEL, IN_AB), D_MODEL ** -0.5)
    w_out_ab = nrm(ks[2], (NE, MIX_AB, D_MODEL), BETA * MIX_AB ** -0.5)
    w_in_cd = nrm(ks[3], (NO, D_MODEL, IN_CD), D_MODEL ** -0.5)
    b_if_cd = jnp.concatenate([
        nrm(ks[4], (NO, N_HEADS_D), 0.1),
        jnp.broadcast_to(jnp.linspace(3.0, 6.0, N_HEADS_D), (NO, N_HEADS_D)) + nrm(ks[5], (NO, N_HEADS_D), 0.1)], axis=-1)
    conv_cd = nrm(ks[6], (NO, CONV_D, 2 * QK_D), CONV_D ** -0.5)
    conv_cd_b = nrm(ks[7], (NO, 2 * QK_D), 0.02)
    norm_cd_g = 1.0 + nrm(ks[8], (NO, W_D), 0.02)
    w_out_cd = nrm(ks[9], (NO, MIX_CD, D_MODEL), BETA * MIX_CD ** -0.5)
    ffn_w_up = nrm(ks[10], (DEPTH, D_MODEL, D_FF), D_MODEL ** -0.5)
    ffn_w_gate = nrm(ks[11], (DEPTH, D_MODEL, D_FF), D_MODEL ** -0.5)
    ffn_conv = nrm(ks[12], (DEPTH, FFN_CONV, D_FF), FFN_CONV ** -0.5)
    ffn_conv_b = nrm(ks[13], (DEPTH, D_FF), 0.02)
    ffn_w_down = nrm(ks[14], (DEPTH, D_FF, D_MODEL), BETA * D_FF ** -0.5)
    ln_g = 1.0 + nrm(ks[15], (DEPTH, 2, D_MODEL), 0.02)
    ln_b = nrm(ks[16], (DEPTH, 2, D_MODEL), 0.02)
    return {"x": x, "w_in_ab": w_in_ab, "w_out_ab": w_out_ab, "w_in_cd": w_in_cd,
            "b_if_cd": b_if_cd, "conv_cd": conv_cd, "conv_cd_b": conv_cd_b, "norm_cd_g": norm_cd_g,
            "w_out_cd": w_out_cd, "ffn_w_up": ffn_w_up, "ffn_w_gate": ffn_w_gate, "ffn_conv": ffn_conv,
            "ffn_conv_b": ffn_conv_b, "ffn_w_down": ffn_w_down, "ln_g": ln_g, "ln_b": ln_b}


def reference(x, w_in_ab, w_out_ab, w_in_cd, b_if_cd, conv_cd, conv_cd_b, norm_cd_g, w_out_cd,
              ffn_w_up, ffn_w_gate, ffn_conv, ffn_conv_b, ffn_w_down, ln_g, ln_b):
    for l in range(DEPTH):
        j = l // 2
        if l % 2 == 0:
            y = mixer_ab(x, w_in_ab[j], w_out_ab[j])
        else:
            y = mixer_cd(x, w_in_cd[j], b_if_cd[j], conv_cd[j], conv_cd_b[j], norm_cd_g[j], w_out_cd[j])
        x = layer_norm(ALPHA * x + y, ln_g[l, 0], ln_b[l, 0])
        y = conv_ffn(x, ffn_w_up[l], ffn_w_gate[l], ffn_conv[l], ffn_conv_b[l], ffn_w_down[l])
        x = layer_norm(ALPHA * x + y, ln_g[l, 1], ln_b[l, 1])
    return x
```

```python
import numpy as np
from contextlib import ExitStack
import concourse.bass as bass
import concourse.mybir as mybir
from concourse.bass_utils import run_bass_kernel_spmd

F32 = mybir.dt.float32
F32R = mybir.dt.float32r
AF = mybir.ActivationFunctionType
ALU = mybir.AluOpType
AX = mybir.AxisListType

S = 2048
D = 4096
DFF = 11008
NFF = 86
ALPHA = 4.0 ** 0.25
EPS = 1e-5
QSCALE = 128.0 ** -0.5
NEG = -200.0
PRJ_ROWS = 97 * 128


class Buf:
    __slots__ = ("w", "r")

    def __init__(self):
        self.w = None
        self.r = {}


class Q:
    def __init__(self, name, sem, is_dma=False, skip_self=False):
        self.name = name
        self.sem = sem
        self.is_dma = is_dma
        self.skip_self = skip_self
        self.count = 0
        self.seen = {}
        self.prog = []


class Sched:
    NP = 40

    def __init__(self, nc, stack):
        mk = lambda n: stack.enter_context(nc.semaphore(n))
        self.pe = Q("pe", mk("s_pe"), skip_self=True)
        self.dve = Q("dve", mk("s_dve"))
        self.act = Q("act", mk("s_act"))
        self.sp = Q("sp", None, is_dma=True)
        self.pool = Q("pool", None, is_dma=True)
        self.queues = [self.pe, self.dve, self.act, self.sp, self.pool]
        self.dsem = [mk("s_d%d" % j) for j in range(self.NP)]
        self.dval = [0] * self.NP
        self.drr = 0

    def op(self, q, emit, reads=(), writes=()):
        deps = {}

        def add(ev):
            if ev is None:
                return
            k, sem, v = ev
            o = deps.get(k)
            if o is None or o[1] < v:
                deps[k] = (sem, v)

        for b in reads:
            add(b.w)
        for b in writes:
            add(b.w)
            for ev in b.r.values():
                add(ev)
        if q.is_dma:
            j = self.drr
            self.drr = (j + 1) % self.NP
            if self.dval[j] > 0:
                add((("d", j), self.dsem[j], self.dval[j]))
            self.dval[j] += 16
            ev = (("d", j), self.dsem[j], self.dval[j])
            inc = 16
        else:
            q.count += 1
            ev = (q.name, q.sem, q.count)
            inc = 1
        for k, (sem, v) in deps.items():
            if k == q.name and q.skip_self:
                continue
            if q.seen.get(k, 0) >= v:
                continue
            q.seen[k] = v
            q.prog.append(lambda e, sem=sem, v=v: e.wait_ge(sem, v))
        sem_ = ev[1]
        q.prog.append(lambda e, emit=emit, sem_=sem_, inc=inc: emit(e).then_inc(sem_, inc))
        for b in reads:
            o = b.r.get(ev[0])
            if o is None or o[2] < ev[2]:
                b.r[ev[0]] = ev
        for b in writes:
            b.w = ev
            b.r = {}
        return ev

    def barrier(self):
        for q in self.queues:
            for q2 in (self.pe, self.dve, self.act):
                if q2 is q:
                    continue
                if q2.count > q.seen.get(q2.name, 0):
                    q.seen[q2.name] = q2.count
                    q.prog.append(lambda e, sem=q2.sem, v=q2.count: e.wait_ge(sem, v))
            for j in range(self.NP):
                v = self.dval[j]
                if v > q.seen.get(("d", j), 0):
                    q.seen[("d", j)] = v
                    q.prog.append(lambda e, sem=self.dsem[j], v=v: e.wait_ge(sem, v))


class DT:
    def __init__(self, name, ap):
        self.name = name
        self.ap = ap


class Builder:
    def __init__(self, dump=(), stop_after=None):
        self.dump = set(dump)
        self.stop_after = stop_after
        self.nc = bass.Bass("TRN2", target_bir_lowering=False)
        self.stack = ExitStack()
        self.dbufs = {}
        self.bank_rr = {}
        self.rot_rr = {}
        self.rots = {}
        self.w_rr = 0
        self.nw = 3
        self.wqs = None
        self.ev_rr = 0
        self.in_names = []

    def db(self, name, idx=0):
        k = (name, idx)
        b = self.dbufs.get(k)
        if b is None:
            b = self.dbufs[k] = Buf()
        return b

    def dram_in(self, name, shape):
        self.in_names.append(name)
        return self.nc.dram_tensor(name, list(shape), F32, kind="ExternalInput").ap()

    def dram_tmp(self, name, shape):
        kind = "ExternalOutput" if name in self.dump else "Internal"
        return DT(name, self.nc.dram_tensor(name, list(shape), F32, kind=kind).ap())

    def sb(self, name, shape, dt=F32):
        return self.stack.enter_context(self.nc.sbuf_tensor(name, list(shape), dt))

    def carve(self, name, off, n, dt=F32, parts=128):
        self.carve_id = getattr(self, "carve_id", 0) + 1
        t = self.nc.alloc_sbuf_tensor_at("%s_c%d" % (name, self.carve_id), [parts, n], dt, offset=self.arena_addr + off * 4)
        return t.ap() if hasattr(t, "ap") and callable(t.ap) else t[:]

    def carve2(self, name, off, n, parts=128):
        return self.carve(name + "F", off, n, F32, parts), self.carve(name + "R", off, n, F32R, parts)

    def next_bank(self, grp):
        lst = self.bank_groups[grp]
        i = self.bank_rr.get(grp, 0)
        self.bank_rr[grp] = (i + 1) % len(lst)
        b = lst[i]
        return self.banks[b], self.bankbufs[b]

    def rot(self, name):
        lst = self.rots[name]
        i = self.rot_rr.get(name, 0)
        self.rot_rr[name] = (i + 1) % len(lst)
        return lst[i]

    def mkrot(self, name, n, shape, dt=F32):
        self.rots[name] = [(self.sb("%s%d" % (name, i), shape, dt), Buf()) for i in range(n)]

    def const_load(self, name, shape, dt=F32):
        src = self.dram_in(name, shape)
        t = self.sb("c_" + name, shape, dt)
        b = Buf()
        s = src.bitcast(F32R) if dt == F32R else src
        self.sc.op(self.sc.pool, lambda e, t=t, s=s: e.dma_start(out=t[:], in_=s), writes=[b])
        return t, b

    def const_into(self, name, shape, dstF, dstR):
        src = self.dram_in(name, shape)
        b = Buf()
        self.sc.op(self.sc.pool, lambda e: e.dma_start(out=dstF, in_=src), writes=[b])
        return dstR, b

    def wload(self, src, n):
        i = self.w_rr % self.nw
        self.w_rr += 1
        slot = self.wslots[i]
        slotF = self.wslotsF[i]
        wqs = self.wqs or [self.sc.sp]
        q = wqs[self.w_rr % len(wqs)]
        self.sc.op(q, lambda e, slotF=slotF, src=src, n=n: e.dma_start(out=slotF[:, 0:n], in_=src),
                   writes=[self.wbufs[i]])
        return slot, self.wbufs[i]

    def proj(self, plan, KC, rhs_fn, N, consumer, grp="proj", kbase=0):
        sc = self.sc
        NH = (N + 511) // 512
        Nh = N // NH
        for (wl, fc, tag) in plan:
            bks = [self.next_bank(grp) for _ in range(NH)]
            k0 = 0
            while k0 < KC:
                n = min(32, KC - k0)
                slot, wbuf = self.wload(wl[fc, :, (kbase + k0) * 128:(kbase + k0 + n) * 128], n * 128)
                for kk in range(n):
                    kc = k0 + kk
                    for hh in range(NH):
                        bank, bbuf = bks[hh]
                        rap, rbuf = rhs_fn(kc, hh)
                        sc.op(sc.pe, lambda e, bank=bank, slot=slot, kk=kk, rap=rap, st=(kc == 0), sp=(kc == KC - 1):
                              e.matmul(bank[:, 0:Nh], lhsT=slot[:, kk * 128:(kk + 1) * 128], rhs=rap, start=st, stop=sp),
                              reads=[wbuf, rbuf], writes=[bbuf])
                k0 += n
            consumer(tag, fc, bks)

    def load_actT(self, src, row0, KC, t0, T, dst, dbufs, q=None):
        sc = self.sc
        q = q or sc.sp
        for kc in range(KC):
            sv = src.ap[row0 + kc * 128: row0 + (kc + 1) * 128, t0:t0 + T]
            dv = dst[:, kc * T:(kc + 1) * T]
            sc.op(q, lambda e, sv=sv, dv=dv: e.dma_start(out=dv, in_=sv),
                  reads=[self.db(src.name, row0 // 128 + kc)], writes=[dbufs[kc]])

    def evac(self, bank, bbuf, N, dst, dstbuf, scale=None, func=None, bias=None, eng=None, extra_reads=()):
        sc = self.sc
        if eng is None:
            if func is not None or bias is not None:
                eng = "act"
            else:
                self.ev_rr ^= 1
                eng = "act" if self.ev_rr else "dve"
        src = bank[:, 0:N]
        if eng == "dve":
            if scale is None:
                sc.op(sc.dve, lambda e: e.tensor_copy(out=dst, in_=src), reads=[bbuf] + list(extra_reads), writes=[dstbuf])
            else:
                sc.op(sc.dve, lambda e: e.tensor_scalar(out=dst, in0=src, scalar1=float(scale), scalar2=None, op0=ALU.mult),
                      reads=[bbuf] + list(extra_reads), writes=[dstbuf])
        else:
            kw = {}
            if scale is not None:
                kw["scale"] = float(scale)
            if bias is not None:
                kw["bias"] = bias
            f = func if func is not None else AF.Copy
            if bias is not None and func is None:
                f = AF.Identity
            sc.op(sc.act, lambda e: e.activation(out=dst, in_=src, func=f, **kw), reads=[bbuf] + list(extra_reads), writes=[dstbuf])

    def build(self):
        nc = self.nc
        st = self.stack
        self.sc = sc = Sched(nc, st)
        self.x_in = self.dram_in("x", [S, D])
        self.out = nc.dram_tensor("out", [S, D], F32, kind="ExternalOutput").ap()
        self.wl_in = [self.dram_in("wl_in0", [96, 128, 4096]), self.dram_in("wl_in1", [97, 128, 4096])]
        self.wl_out = [self.dram_in("wl_out0", [32, 128, 4096]), self.dram_in("wl_out1", [32, 128, 4096])]
        self.wl_g = [self.dram_in("wl_g%d" % l, [NFF, 128, 4096]) for l in range(2)]
        self.wl_u = [self.dram_in("wl_u%d" % l, [NFF, 128, 4096]) for l in range(2)]
        self.wl_d = [self.dram_in("wl_d%d" % l, [32, 128, NFF * 128]) for l in range(2)]
        self.XT = self.dram_tmp("XT", [D, S])
        self.PRJ = self.dram_tmp("PRJ", [PRJ_ROWS, S])
        self.MIXT = self.dram_tmp("MIXT", [D, S])
        self.X1T = self.dram_tmp("X1T", [D, S])
        self.HT = self.dram_tmp("HT", [DFF, S])

        AREN = 46336
        self.arena_t = self.sb("arena", [128, AREN], F32)
        self.arena_addr = None
        for al in nc.allocations:
            if getattr(al, "name", None) == "arena_set":
                self.arena_addr = al.memorylocations[0].addr
        assert self.arena_addr is not None
        ws = [self.carve2("wslot%d" % i, 34048 + i * 4096, 4096) for i in range(3)]
        self.wslotsF = [w[0] for w in ws]
        self.wslots = [w[1] for w in ws]
        self.wbufs = [Buf() for _ in range(3)]
        self.banks = [st.enter_context(nc.psum_tensor("ps%d" % i, [128, 512], F32)) for i in range(8)]
        self.bankbufs = [Buf() for _ in range(8)]
        self.bank_groups = {"proj": [0, 1, 2, 3], "aux": [4, 5, 6, 7]}
        self.mkrot("stage", 4, [128, 512])
        cF, cR = self.carve2("cst", 32768, 1280)
        self.cstF = cF
        self.ident, self.identb = self.const_load("ident", [128, 128])
        self.identr, self.identrb = self.const_into("identr", [128, 128], cF[:, 0:128], cR[:, 0:128])
        self.onesr, self.onesrb = self.const_into("onesr", [128, 128], cF[:, 128:256], cR[:, 128:256])
        self.onesD, self.onesDb = self.const_load("onesD", [128, 128])
        self.lnp, self.lnpb = self.const_load("lnp", [128, 256])
        self.ffc, self.ffcb = self.const_load("ffc", [128, 2 * NFF * 4])
        self.vm, self.vmb = self.const_load("vm", [128, 128])
        self.own, self.ownb = self.const_load("own", [128, 128])
        self.en, self.enb = self.const_into("en", [128, 1024], cF[:, 256:1280], cR[:, 256:1280])
        self.rm, self.rmb = self.const_load("rm", [128, 16])
        self.lma_d = self.dram_in("lma", [128, 16 * 512])
        self.lmb_d = self.dram_in("lmb", [128, 4 * 512])

        self.tri, self.trib = self.const_load("tri", [128, 128])
        self.ones, self.onesb = self.const_load("ones", [128, 128])
        self.o256, self.o256b = self.const_load("o256", [128, 128])
        self.cdc, self.cdcb = self.const_load("cdc", [128, 80])
        self.ngc, self.ngcb = self.const_load("ngc", [128, 16])
        self.bif, self.bifb = self.const_load("bif", [128, 1])
        self.lmc_d = self.dram_in("lmc", [128, 4 * 512])
        self.trige_d = self.dram_in("trige", [128, 128])
        l0only = self.stop_after in ("P0", "P1", "P2", "P3a", "P3b", "P3c")
        phases = [("P0", self.phase_transpose_in),
                  ("P1", lambda: self.phase_in_proj(0)),
                  ("P2", self.phase_attn_ab),
                  ("P3a", lambda: self.phase_proj_ln(0, 0)),
                  ("P3b", lambda: self.phase_ffn_upgate(0)),
                  ("P3c", lambda: self.phase_proj_ln(0, 1, final=l0only)),
                  ("Q1", lambda: self.phase_in_proj(1)),
                  ("Q2c", self.phase_attn_c),
                  ("Q2d", self.phase_mlstm),
                  ("Q3a", lambda: self.phase_proj_ln(1, 0)),
                  ("Q3b", lambda: self.phase_ffn_upgate(1)),
                  ("Q3c", lambda: self.phase_proj_ln(1, 1, final=True)),
                  ]
        for name, fn in phases:
            fn()
            sc.barrier()
            if self.stop_after == name:
                break
        if self.stop_after is not None and self.stop_after not in ("P3c", "Q3c"):
            stg, sb_ = self.rot("stage")
            sc.op(sc.dve, lambda e: e.memset(stg[:], 0.0), writes=[sb_])
            sc.op(sc.pool, lambda e: e.dma_start(out=self.out[0:128, 0:512], in_=stg[:]), reads=[sb_], writes=[self.db("out", 0)])
            sc.barrier()
        with nc.Block() as block:
            @block.tensor
            def _(e):
                for f in sc.pe.prog:
                    f(e)

            @block.vector
            def _(e):
                for f in sc.dve.prog:
                    f(e)

            @block.scalar
            def _(e):
                for f in sc.act.prog:
                    f(e)

            @block.gpsimd
            def _(e):
                for f in sc.pool.prog:
                    f(e)

            @block.sync
            def _(e):
                for f in sc.sp.prog:
                    f(e)
        return nc

    def phase_transpose_in(self):
        sc = self.sc
        AFv = self.carve("p0x", 0, 32768, F32)
        abufs = [[Buf() for _ in range(4)] for _ in range(2)]
        for tt in range(4):
            half = tt % 2
            base = half * 16384
            for ts in range(4):
                r0 = (tt * 4 + ts) * 128
                sc.op(sc.sp, lambda e, base=base, ts=ts, r0=r0: e.dma_start(
                    out=AFv[:, base + ts * 4096: base + (ts + 1) * 4096], in_=self.x_in[r0:r0 + 128, :]),
                    writes=[abufs[half][ts]])
            for kc in range(32):
                bank, bb = self.next_bank("aux")
                for ts in range(4):
                    o = base + ts * 4096 + kc * 128
                    sc.op(sc.pe, lambda e, bank=bank, ts=ts, o=o: e.transpose(
                        out=bank[:, ts * 128:(ts + 1) * 128], in_=AFv[:, o:o + 128], identity=self.ident[:]),
                        reads=[abufs[half][ts], self.identb], writes=[bb])
                stg, sb_ = self.rot("stage")
                self.evac(bank, bb, 512, stg[:], sb_)
                sc.op(sc.pool, lambda e, stg=stg, kc=kc, tt=tt: e.dma_start(
                    out=self.XT.ap[kc * 128:(kc + 1) * 128, tt * 512:(tt + 1) * 512], in_=stg[:]),
                    reads=[sb_], writes=[self.db("XT", kc)])

    def phase_in_proj(self, l):
        sc = self.sc
        self.wqs = None
        self.nw = 2
        wl = self.wl_in[l]
        FC = 96 if l == 0 else 97
        T = 1024
        abufs = [Buf() for _ in range(32)]
        xF, xR = self.carve2("p1x", 0, 32768)
        if l == 1:
            F1 = self.carve("p1f", 42240, 4096, F32)
            halo = F1[:, 0:48]
            halob = Buf()
            gbufs = [(F1[:, 64 + i * 520: 64 + i * 520 + 515], Buf()) for i in range(2)]
            accs = [(F1[:, 1200 + i * 512: 1200 + (i + 1) * 512], Buf()) for i in range(2)]
            sc.op(sc.dve, lambda e: e.memset(halo, 0.0), writes=[halob])
            rr = [0]
        for tt in range(S // T):
            self.load_actT(self.XT, 0, 32, tt * T, T, xF, abufs)
            rhs_fn = lambda kc, hh: (xR[:, kc * T + hh * 512: kc * T + (hh + 1) * 512], abufs[kc])

            def consumer(tag, fc, bks, tt=tt):
                for hh, (bank, bb) in enumerate(bks):
                    stg, sb_ = self.rot("stage")
                    if l == 0:
                        scale = QSCALE if (fc < 16 or 48 <= fc < 64) else None
                        self.evac(bank, bb, 512, stg[:], sb_, scale=scale)
                    elif fc < 16:
                        self.evac(bank, bb, 512, stg[:], sb_, scale=QSCALE)
                    elif fc < 48 or 64 <= fc < 80:
                        self.evac(bank, bb, 512, stg[:], sb_)
                    elif fc < 64:
                        ci = fc - 48
                        c0 = ci * 5
                        cw = self.cdc
                        i = rr[0] % 2
                        rr[0] += 1
                        gb, gbb = gbufs[i]
                        acc, accb = accs[i]
                        sc.op(sc.act, lambda e, gb=gb, ci=ci: e.activation(out=gb[:, 0:3], in_=halo[:, ci * 3:ci * 3 + 3], func=AF.Copy),
                              reads=[halob], writes=[gbb])
                        sc.op(sc.dve, lambda e, gb=gb, bank=bank: e.tensor_copy(out=gb[:, 3:515], in_=bank[:, 0:512]), reads=[bb], writes=[gbb])
                        sc.op(sc.act, lambda e, gb=gb, ci=ci: e.activation(out=halo[:, ci * 3:ci * 3 + 3], in_=gb[:, 512:515], func=AF.Copy),
                              reads=[gbb], writes=[halob])
                        sc.op(sc.dve, lambda e, gb=gb, acc=acc, c0=c0: e.tensor_scalar(out=acc, in0=gb[:, 3:515], scalar1=cw[:, c0 + 3:c0 + 4],
                                                                                      scalar2=cw[:, c0 + 4:c0 + 5], op0=ALU.mult, op1=ALU.add),
                              reads=[gbb, self.cdcb], writes=[accb])
                        for k in (2, 1, 0):
                            sc.op(sc.dve, lambda e, k=k, gb=gb, acc=acc, c0=c0: e.scalar_tensor_tensor(
                                out=acc, in0=gb[:, k:k + 512], scalar=cw[:, c0 + k:c0 + k + 1], in1=acc, op0=ALU.mult, op1=ALU.add),
                                reads=[gbb, self.cdcb, accb], writes=[accb])
                        if fc < 56:
                            sc.op(sc.act, lambda e, acc=acc, stg=stg: e.activation(out=stg[:], in_=acc, func=AF.Silu), reads=[accb], writes=[sb_])
                        else:
                            sc.op(sc.act, lambda e, acc=acc: e.activation(out=acc, in_=acc, func=AF.Silu), reads=[accb], writes=[accb])
                            sc.op(sc.dve, lambda e, acc=acc, stg=stg: e.tensor_scalar(out=stg[:], in0=acc, scalar1=float(QSCALE), scalar2=None, op0=ALU.mult),
                                  reads=[accb], writes=[sb_])
                    elif fc < 96:
                        self.evac(bank, bb, 512, stg[:], sb_, func=AF.Sigmoid)
                    else:
                        self.evac(bank, bb, 512, stg[:], sb_, bias=self.bif[:, 0:1], extra_reads=[self.bifb])
                    sc.op(sc.pool, lambda e, stg=stg, fc=fc, hh=hh: e.dma_start(
                        out=self.PRJ.ap[fc * 128:(fc + 1) * 128, tt * T + hh * 512: tt * T + (hh + 1) * 512], in_=stg[:]),
                        reads=[sb_], writes=[self.db("PRJ", fc)])

            self.proj([(wl, fc, "p") for fc in range(FC)], 32, rhs_fn, T, consumer)
        self.nw = 3

    def phase_attn_ab(self):
        sc = self.sc
        R1F, R1 = self.carve2("p2r", 0, 32768)
        F1 = self.carve("p2f", 34048, 2048, F32)
        QT = [R1[:, 0:2048], R1[:, 6144:8192]]
        KT = [R1[:, 2048:4096], R1[:, 8192:10240]]
        QTF = [R1F[:, 0:2048], R1F[:, 6144:8192]]
        KTF = [R1F[:, 2048:4096], R1F[:, 8192:10240]]
        VTF = [R1F[:, 4096:6144], R1F[:, 10240:12288]]
        qtb, ktb, vtb = [Buf(), Buf()], [Buf(), Buf()], [Buf(), Buf()]
        V = [R1[:, 12288:14336], R1[:, 14336:16384]]
        VF = [R1F[:, 12288:14336], R1F[:, 14336:16384]]
        vb = [Buf(), Buf()]
        Pb = [(R1[:, 16384 + i * 512: 16384 + (i + 1) * 512], R1F[:, 16384 + i * 512: 16384 + (i + 1) * 512], Buf()) for i in range(3)]
        LMA = R1[:, 17920:17920 + 8192]
        LMB = R1[:, 26112:26112 + 2048]
        lmab, lmbb = Buf(), Buf()
        biasT = [R1[:, 28160:30208], R1[:, 30208:32256]]
        biasTF = [R1F[:, 28160:30208], R1F[:, 30208:32256]]
        biasTb = [Buf(), Buf()]
        kmT = F1[:, 0:8]
        Gm = F1[:, 128:256]
        top8 = F1[:, 256:384]
        thr = F1[:, 384:400]
        bias = F1[:, 512:640]
        rden = F1[:, 1024:1536]
        kmb, gmb, t8b, thrb, biasb, rdb = Buf(), Buf(), Buf(), Buf(), Buf(), Buf()
        sc.op(sc.pool, lambda e: e.dma_start(out=R1F[:, 17920:17920 + 8192], in_=self.lma_d), writes=[lmab])
        sc.op(sc.pool, lambda e: e.dma_start(out=R1F[:, 26112:26112 + 2048], in_=self.lmb_d), writes=[lmbb])
        SB = [0, 1, 4]
        ND = [(2, 3), (5, 6)]
        AUX = 7

        def pre(hh):
            isB = hh >= 16
            h = hh % 16
            par = hh % 2
            qrow = (48 + h) * 128 if isB else h * 128
            krow = qrow + 16 * 128
            vrow = qrow + 32 * 128
            for (dstt, row, bb_) in ((QTF[par], qrow, qtb[par]), (KTF[par], krow, ktb[par]), (VTF[par], vrow, vtb[par])):
                sc.op(sc.sp, lambda e, dstt=dstt, row=row: e.dma_start(out=dstt, in_=self.PRJ.ap[row:row + 128, :]),
                      reads=[self.db("PRJ", row // 128)], writes=[bb_])
            bank, bb = self.banks[AUX], self.bankbufs[AUX]
            for g4 in range(4):
                for j in range(4):
                    kj = g4 * 4 + j
                    sc.op(sc.pe, lambda e, j=j, kj=kj: e.transpose(
                        out=bank[:, j * 128:(j + 1) * 128], in_=VTF[par][:, kj * 128:(kj + 1) * 128], identity=self.ident[:]),
                        reads=[vtb[par], self.identb], writes=[bb])
                self.evac(bank, bb, 512, VF[par][:, g4 * 512:(g4 + 1) * 512], vb[par])
            if isB:
                sc.op(sc.dve, lambda e: e.tensor_reduce(
                    out=kmT, in_=KTF[par].rearrange("p (n s) -> p n s", s=256), axis=AX.X, op=ALU.add),
                    reads=[ktb[par]], writes=[kmb])
                for i in range(16):
                    sc.op(sc.pe, lambda e, i=i: e.matmul(
                        bank[:, i * 8:(i + 1) * 8], lhsT=QTF[par][:, i * 128:(i + 1) * 128], rhs=kmT, start=True, stop=True),
                        reads=[qtb[par], kmb], writes=[bb])
                sc.op(sc.dve, lambda e: e.tensor_tensor(out=Gm, in0=bank[:, 0:128], in1=self.vm[:], op=ALU.add),
                      reads=[bb, self.vmb], writes=[gmb])
                for i in range(16):
                    sc.op(sc.dve, lambda e, i=i: e.max(out=top8[:, i * 8:(i + 1) * 8], in_=Gm[:, i * 8:(i + 1) * 8]),
                          reads=[gmb], writes=[t8b])
                sc.op(sc.dve, lambda e: e.tensor_scalar(
                    out=thr, in0=top8.rearrange("p (i k) -> p i k", k=8)[:, :, 2], scalar1=-1e29, scalar2=None, op0=ALU.max),
                    reads=[t8b], writes=[thrb])
                for i in range(16):
                    sc.op(sc.dve, lambda e, i=i: e.tensor_scalar(
                        out=bias[:, i * 8:(i + 1) * 8], in0=Gm[:, i * 8:(i + 1) * 8], scalar1=thr[:, i:i + 1], scalar2=-1.0,
                        op0=ALU.is_ge, op1=ALU.add), reads=[gmb, thrb], writes=[biasb])
                sc.op(sc.dve, lambda e: e.tensor_tensor(out=bias, in0=bias, in1=self.own[:], op=ALU.add),
                      reads=[biasb, self.ownb], writes=[biasb])
                sc.op(sc.pe, lambda e: e.transpose(out=bank[:, 128:256], in_=bias, identity=self.ident[:]),
                      reads=[biasb, self.identb], writes=[bb])
                for i in range(16):
                    sc.op(sc.dve, lambda e, i=i: e.tensor_scalar(out=biasTF[par][:, i * 128:(i + 1) * 128], in0=bank[:, 128:256],
                                                                scalar1=self.rm[:, i:i + 1], scalar2=None, op0=ALU.mult),
                          reads=[bb, self.rmb], writes=[biasTb[par]])

        steps = []
        gi = 0
        for hh in range(32):
            for g in range(4):
                for kj in range(4 * g + 4):
                    steps.append((hh, g, kj, gi))
                gi += 1

        def front(idx):
            hh, g, kj, G = steps[idx]
            isB = hh >= 16
            par = hh % 2
            sb_i = SB[idx % 3]
            sbank, sbb = self.banks[sb_i], self.bankbufs[sb_i]
            P, PF, Pbb = Pb[idx % 3]
            diag = kj >= 4 * g
            cs = max(0, kj - 4 * g) * 128
            sc.op(sc.pe, lambda e: e.matmul(
                sbank[:, cs:512], lhsT=KT[par][:, kj * 128:(kj + 1) * 128], rhs=QT[par][:, g * 512 + cs:(g + 1) * 512], start=True, stop=False),
                reads=[ktb[par], qtb[par]], writes=[sbb])
            if not isB:
                dd = 4 * g - kj + 3
                sc.op(sc.pe, lambda e: e.matmul(
                    sbank[:, cs:512], lhsT=self.identr[:], rhs=LMA[:, dd * 512 + cs:(dd + 1) * 512], start=False, stop=True),
                    reads=[self.identrb, lmab], writes=[sbb])
            else:
                n = kj // 2
                sc.op(sc.pe, lambda e: e.matmul(
                    sbank[:, cs:512], lhsT=self.en[:, n * 128:(n + 1) * 128], rhs=biasT[par][:, g * 512 + cs:(g + 1) * 512],
                    start=False, stop=(not diag)), reads=[self.enb, biasTb[par]], writes=[sbb])
                if diag:
                    jj = kj - 4 * g
                    sc.op(sc.pe, lambda e: e.matmul(
                        sbank[:, cs:512], lhsT=self.identr[:], rhs=LMB[:, jj * 512 + cs:(jj + 1) * 512], start=False, stop=True),
                        reads=[self.identrb, lmbb], writes=[sbb])
            sc.op(sc.act, lambda e: e.activation(out=PF[:, cs:512], in_=sbank[:, cs:512], func=AF.Exp), reads=[sbb], writes=[Pbb])

        def back(idx):
            hh, g, kj, G = steps[idx]
            par = hh % 2
            P, PF, Pbb = Pb[idx % 3]
            nbi, dbi = ND[G % 2]
            nbank, nbb = self.banks[nbi], self.bankbufs[nbi]
            dbank, dbb = self.banks[dbi], self.bankbufs[dbi]
            nk = 4 * g + 4
            cs = max(0, kj - 4 * g) * 128
            sc.op(sc.pe, lambda e: e.matmul(
                nbank[:, cs:512], lhsT=V[par][:, kj * 128:(kj + 1) * 128], rhs=P[:, cs:512], start=(kj == 0), stop=(kj == nk - 1)),
                reads=[vb[par], Pbb], writes=[nbb])
            sc.op(sc.pe, lambda e: e.matmul(
                dbank[:, cs:512], lhsT=self.onesr[:], rhs=P[:, cs:512], start=(kj == 0), stop=(kj == nk - 1)),
                reads=[self.onesrb, Pbb], writes=[dbb])
            if kj == nk - 1:
                sc.op(sc.dve, lambda e: e.reciprocal(out=rden, in_=dbank[:, 0:512]), reads=[dbb], writes=[rdb])
                stg, sb_ = self.rot("stage")
                sc.op(sc.dve, lambda e: e.tensor_tensor(out=stg[:], in0=nbank[:, 0:512], in1=rden, op=ALU.mult),
                      reads=[nbb, rdb], writes=[sb_])
                sc.op(sc.pool, lambda e: e.dma_start(
                    out=self.MIXT.ap[hh * 128:(hh + 1) * 128, g * 512:(g + 1) * 512], in_=stg[:]),
                    reads=[sb_], writes=[self.db("MIXT", hh)])

        LA = 2
        n = len(steps)
        pre(0)
        for idx in range(n + LA):
            if idx < n:
                hh, g, kj, G = steps[idx]
                if g == 3 and kj == 0 and hh + 1 < 32:
                    pre(hh + 1)
                front(idx)
            if idx - LA >= 0:
                back(idx - LA)

    def phase_attn_c(self):
        sc = self.sc
        R1F, R1 = self.carve2("pcr", 0, 24576)
        F1 = self.carve("pcf", 24576, 4096, F32)
        QT = [R1[:, 0:2048], R1[:, 6144:8192]]
        KT = [R1[:, 2048:4096], R1[:, 8192:10240]]
        QTF = [R1F[:, 0:2048], R1F[:, 6144:8192]]
        KTF = [R1F[:, 2048:4096], R1F[:, 8192:10240]]
        VTF = [R1F[:, 4096:6144], R1F[:, 10240:12288]]
        qtb, ktb, vtb = [Buf(), Buf()], [Buf(), Buf()], [Buf(), Buf()]
        V = [R1[:, 12288:14336], R1[:, 14336:16384]]
        VF = [R1F[:, 12288:14336], R1F[:, 14336:16384]]
        vb = [Buf(), Buf()]
        SP = [(R1[:, 16384 + i * 512:16384 + (i + 1) * 512], R1F[:, 16384 + i * 512:16384 + (i + 1) * 512], Buf()) for i in range(3)]
        AT = [(R1[:, 17920 + i * 512:17920 + (i + 1) * 512], R1F[:, 17920 + i * 512:17920 + (i + 1) * 512], Buf()) for i in range(3)]
        RACC, RACCF, raccb = R1[:, 19456:19968], R1F[:, 19456:19968], Buf()
        LMC, lmcb = R1[:, 19968:19968 + 2048], Buf()
        TRI, trigb = R1[:, 22016:22144], Buf()
        EZ = [(F1[:, i * 512:(i + 1) * 512], Buf()) for i in range(4)]
        E2 = [(F1[:, 2048 + i * 512:2048 + (i + 1) * 512], Buf()) for i in range(2)]
        sc.op(sc.pool, lambda e: e.dma_start(out=R1F[:, 19968:19968 + 2048], in_=self.lmc_d), writes=[lmcb])
        sc.op(sc.pool, lambda e: e.dma_start(out=R1F[:, 22016:22144], in_=self.trige_d), writes=[trigb])
        ZB = [0, 1, 4]
        RB = [6, 7, 5]
        OB = 2
        AUX = 3

        def pre(h):
            par = h % 2
            for (dstt, row, bb_) in ((QTF[par], h * 128, qtb[par]), (KTF[par], (16 + h) * 128, ktb[par]), (VTF[par], (32 + h) * 128, vtb[par])):
                sc.op(sc.sp, lambda e, dstt=dstt, row=row: e.dma_start(out=dstt, in_=self.PRJ.ap[row:row + 128, :]),
                      reads=[self.db("PRJ", row // 128)], writes=[bb_])
            bank, bb = self.banks[AUX], self.bankbufs[AUX]
            for g4 in range(4):
                for j in range(4):
                    kj = g4 * 4 + j
                    sc.op(sc.pe, lambda e, j=j, kj=kj: e.transpose(
                        out=bank[:, j * 128:(j + 1) * 128], in_=VTF[par][:, kj * 128:(kj + 1) * 128], identity=self.ident[:]),
                        reads=[vtb[par], self.identb], writes=[bb])
                self.evac(bank, bb, 512, VF[par][:, g4 * 512:(g4 + 1) * 512], vb[par])

        steps = []
        for h in range(16):
            for g in range(4):
                for kj in range(4 * g + 3, -1, -1):
                    steps.append((h, g, kj))

        def stF(idx):
            h, g, kj = steps[idx]
            par = h % 2
            zi = ZB[idx % 3]
            zbank, zbb = self.banks[zi], self.bankbufs[zi]
            ez, ezb = EZ[idx % 4]
            sp, spF, spb = SP[idx % 3]
            diag = kj >= 4 * g
            cs = max(0, kj - 4 * g) * 128
            sc.op(sc.pe, lambda e: e.matmul(
                zbank[:, cs:512], lhsT=KT[par][:, kj * 128:(kj + 1) * 128], rhs=QT[par][:, g * 512 + cs:(g + 1) * 512], start=True, stop=(not diag)),
                reads=[ktb[par], qtb[par]], writes=[zbb])
            if diag:
                jj = kj - 4 * g
                sc.op(sc.pe, lambda e: e.matmul(
                    zbank[:, cs:512], lhsT=self.identr[:], rhs=LMC[:, jj * 512 + cs:(jj + 1) * 512], start=False, stop=True),
                    reads=[self.identrb, lmcb], writes=[zbb])
            sc.op(sc.act, lambda e: e.activation(out=ez[:, cs:512], in_=zbank[:, cs:512], func=AF.Exp), reads=[zbb], writes=[ezb])
            sc.op(sc.act, lambda e: e.activation(out=spF[:, cs:512], in_=ez[:, cs:512], func=AF.Ln, bias=1.0), reads=[ezb], writes=[spb])

        def stM(idx):
            h, g, kj = steps[idx]
            first = (kj == 4 * g + 3)
            cs = max(0, kj - 4 * g) * 128
            ri = RB[idx % 3]
            rbank, rbb = self.banks[ri], self.bankbufs[ri]
            ez, ezb = EZ[idx % 4]
            sp, spF, spb = SP[idx % 3]
            at, atF, atb = AT[idx % 3]
            e2, e2b = E2[idx % 2]
            if first:
                sc.op(sc.dve, lambda e: e.memset(RACCF[:, 0:384], 0.0), writes=[raccb])
            sc.op(sc.pe, lambda e: e.matmul(rbank[:, cs:512], lhsT=TRI, rhs=sp[:, cs:512], start=True, stop=first), reads=[trigb, spb], writes=[rbb])
            if not first:
                sc.op(sc.pe, lambda e: e.matmul(rbank[:, cs:512], lhsT=self.onesr[:], rhs=RACC[:, cs:512], start=False, stop=True),
                      reads=[self.onesrb, raccb], writes=[rbb])
            if kj > 0:
                if first:
                    sc.op(sc.dve, lambda e: e.tensor_copy(out=RACCF[:, cs:512], in_=spF[:, cs:512]), reads=[spb], writes=[raccb])
                else:
                    sc.op(sc.dve, lambda e: e.tensor_tensor(out=RACCF[:, cs:512], in0=RACCF[:, cs:512], in1=spF[:, cs:512], op=ALU.add),
                          reads=[spb, raccb], writes=[raccb])
            sc.op(sc.act, lambda e: e.activation(out=e2[:, cs:512], in_=rbank[:, cs:512], func=AF.Exp, scale=-1.0), reads=[rbb], writes=[e2b])
            sc.op(sc.dve, lambda e: e.tensor_tensor(out=atF[:, cs:512], in0=ez[:, cs:512], in1=e2[:, cs:512], op=ALU.mult), reads=[ezb, e2b], writes=[atb])

        def stB(idx):
            h, g, kj = steps[idx]
            par = h % 2
            first = (kj == 4 * g + 3)
            cs = max(0, kj - 4 * g) * 128
            at, atF, atb = AT[idx % 3]
            obank, obb = self.banks[OB], self.bankbufs[OB]
            sc.op(sc.pe, lambda e: e.matmul(
                obank[:, cs:512], lhsT=V[par][:, kj * 128:(kj + 1) * 128], rhs=at[:, cs:512], start=first, stop=(kj == 0)),
                reads=[vb[par], atb], writes=[obb])
            if kj == 0:
                stg, sb_ = self.rot("stage")
                self.evac(obank, obb, 512, stg[:], sb_)
                sc.op(sc.pool, lambda e: e.dma_start(
                    out=self.MIXT.ap[h * 128:(h + 1) * 128, g * 512:(g + 1) * 512], in_=stg[:]),
                    reads=[sb_], writes=[self.db("MIXT", h)])

        n = len(steps)
        pre(0)
        for idx in range(n + 2):
            if idx < n:
                h, g, kj = steps[idx]
                if g == 3 and kj == 4 * g + 3 and h + 1 < 16:
                    pre(h + 1)
                stF(idx)
            if 0 <= idx - 1 < n:
                stM(idx - 1)
            if 0 <= idx - 2 < n:
                stB(idx - 2)

    def phase_mlstm(self):
        sc = self.sc
        RF, R = self.carve2("pdr", 0, 20480)
        F1 = self.carve("pdf", 20480, 12288, F32)
        QT, QTF, qtb = R[:, 0:2048], RF[:, 0:2048], Buf()
        KT, KTF, ktb = R[:, 2048:4096], RF[:, 2048:4096], Buf()
        KK, KKF, kkb = R[:, 4096:6144], RF[:, 4096:6144], Buf()
        CT, CTF, ctb = R[:, 6144:6400], RF[:, 6144:6400], Buf()
        NB, NBF, nbb = R[:, 6400:6528], RF[:, 6400:6528], Buf()
        VA = [(R[:, 6528 + i * 256:6528 + (i + 1) * 256], RF[:, 6528 + i * 256:6528 + (i + 1) * 256], Buf()) for i in range(2)]
        ABC = [(R[:, 7040 + i * 128:7040 + (i + 1) * 128], RF[:, 7040 + i * 128:7040 + (i + 1) * 128], Buf()) for i in range(2)]
        MT = [(R[:, 7296 + i * 128:7296 + (i + 1) * 128], RF[:, 7296 + i * 128:7296 + (i + 1) * 128], Buf()) for i in range(2)]
        VDT, vdb = F1[:, 0:4096], [Buf(), Buf()]
        ODT, odb = F1[:, 4096:8192], [Buf(), Buf()]
        VV, vvb = RF[:, 8192:12288], Buf()
        YT, ytb = RF[:, 12288:16384], [Buf(), Buf()]
        G16, g16b = F1[0:16, 8192:8192 + 2048], Buf()
        GT, gtb = F1[:, 10240:10496], Buf()
        LF, lfb = F1[:, 10496:10752], Buf()
        SM = F1[:, 10752:12288]
        LFB, lfbb = SM[:, 0:128], Buf()
        acol, acb = SM[:, 128:129], Buf()
        egc, egb = SM[:, 129:130], Buf()
        dcol, dcb = SM[:, 130:131], Buf()
        EB, ebb = SM[:, 256:384], Buf()
        DN, dnb = SM[:, 384:512], Buf()
        WW, wwb = SM[:, 512:640], Buf()
        HH, hhb = SM[:, 640:896], Buf()
        SQ, sqb = SM[:, 896:1152], Buf()
        MU, mub = SM[:, 1152:1280], Buf()
        RS, rsb = SM[:, 1280:1408], Buf()
        TM, tmb = SM[:, 1408:1536], Buf()
        LN4 = RF[:, 16384:16384 + 2560]
        SQ4 = LN4[:, 0:1024]
        MU4 = LN4[:, 1024:1536]
        RS4 = LN4[:, 1536:2048]
        TM4 = LN4[:, 2048:2560]
        sc.op(sc.sp, lambda e: e.dma_start(out=G16, in_=self.PRJ.ap[96 * 128:96 * 128 + 16, :]), reads=[self.db("PRJ", 96)], writes=[g16b])
        bank, bb = self.next_bank("aux")
        for c in range(16):
            sc.op(sc.pe, lambda e, c=c, bank=bank: e.transpose(out=bank[:, c * 16:(c + 1) * 16], in_=G16[:, c * 128:(c + 1) * 128],
                                                              identity=self.ident[0:16, 0:16]), reads=[g16b, self.identb], writes=[bb])
        sc.op(sc.dve, lambda e: e.tensor_copy(out=GT, in_=bank[:, 0:256]), reads=[bb], writes=[gtb])
        sc.op(sc.act, lambda e: e.activation(out=LF, in_=GT, func=AF.Exp, scale=-1.0), reads=[gtb], writes=[lfb])
        sc.op(sc.act, lambda e: e.activation(out=LF, in_=LF, func=AF.Ln, bias=1.0), reads=[lfb], writes=[lfb])
        sc.op(sc.dve, lambda e: e.tensor_scalar(out=LF, in0=LF, scalar1=-1.0, scalar2=None, op0=ALU.mult), reads=[lfb], writes=[lfb])
        rr = 0
        for h in range(8):
            sc.op(sc.sp, lambda e, h=h: e.dma_start(out=QTF, in_=self.PRJ.ap[(48 + h) * 128:(49 + h) * 128, :]),
                  reads=[self.db("PRJ", 48 + h)], writes=[qtb])
            sc.op(sc.sp, lambda e, h=h: e.dma_start(out=KTF, in_=self.PRJ.ap[(56 + h) * 128:(57 + h) * 128, :]),
                  reads=[self.db("PRJ", 56 + h)], writes=[ktb])
            for j in range(2):
                sc.op(sc.sp, lambda e, h=h, j=j: e.dma_start(out=VDT[:, j * 2048:(j + 1) * 2048],
                                                             in_=self.PRJ.ap[(64 + 2 * h + j) * 128:(65 + 2 * h + j) * 128, :]),
                      reads=[self.db("PRJ", 64 + 2 * h + j)], writes=[vdb[j]])
                sc.op(sc.sp, lambda e, h=h, j=j: e.dma_start(out=ODT[:, j * 2048:(j + 1) * 2048],
                                                             in_=self.PRJ.ap[(80 + 2 * h + j) * 128:(81 + 2 * h + j) * 128, :]),
                      reads=[self.db("PRJ", 80 + 2 * h + j)], writes=[odb[j]])
            for g4 in range(4):
                bank, bb = self.next_bank("aux")
                for jj in range(4):
                    c = g4 * 4 + jj
                    sc.op(sc.pe, lambda e, bank=bank, jj=jj, c=c: e.transpose(
                        out=bank[:, jj * 128:(jj + 1) * 128], in_=KTF[:, c * 128:(c + 1) * 128], identity=self.ident[:]),
                        reads=[ktb, self.identb], writes=[bb])
                self.evac(bank, bb, 512, KKF[:, g4 * 512:(g4 + 1) * 512], kkb)
            for c2 in range(8):
                bank, bb = self.next_bank("aux")
                for jj in range(4):
                    c = c2 * 2 + jj // 2
                    j = jj % 2
                    sc.op(sc.pe, lambda e, bank=bank, jj=jj, c=c, j=j: e.transpose(
                        out=bank[:, jj * 128:(jj + 1) * 128], in_=VDT[:, j * 2048 + c * 128: j * 2048 + (c + 1) * 128], identity=self.ident[:]),
                        reads=[vdb[j], self.identb], writes=[bb])
                self.evac(bank, bb, 512, VV[:, c2 * 512:(c2 + 1) * 512], vvb)
            sc.op(sc.dve, lambda e: e.memset(CTF, 0.0), writes=[ctb])
            sc.op(sc.dve, lambda e: e.memset(NBF, 0.0), writes=[nbb])
            for c in range(16):
                i = rr % 2
                rr += 1
                va, vaF, vab = VA[i]
                abc, abcF, abcb = ABC[i]
                mt, mtF, mtb = MT[i]
                lic = GT[:, c * 16 + h:c * 16 + h + 1]
                lfc = LF[:, c * 16 + 8 + h:c * 16 + 8 + h + 1]
                b1, b1b = self.banks[4], self.bankbufs[4]
                sc.op(sc.pe, lambda e, lfc=lfc: e.matmul(b1[:, 0:1], lhsT=self.tri[:], rhs=lfc, start=True, stop=True),
                      reads=[self.trib, lfb], writes=[b1b])
                sc.op(sc.dve, lambda e, lfc=lfc: e.tensor_scalar(out=LFB, in0=self.ones[:], scalar1=lfc, scalar2=None, op0=ALU.mult),
                      reads=[self.onesb, lfb], writes=[lfbb])
                sc.op(sc.pe, lambda e: e.matmul(b1[:, 128:256], lhsT=LFB, rhs=self.tri[:], start=True, stop=True),
                      reads=[lfbb, self.trib], writes=[b1b])
                sc.op(sc.dve, lambda e, lic=lic: e.tensor_tensor(out=dcol, in0=lic, in1=b1[:, 0:1], op=ALU.subtract),
                      reads=[gtb, b1b], writes=[dcb])
                sc.op(sc.act, lambda e: e.activation(out=acol, in_=dcol, func=AF.Exp), reads=[dcb], writes=[acb])
                sc.op(sc.act, lambda e: e.activation(out=EB, in_=b1[:, 128:256], func=AF.Exp), reads=[b1b], writes=[ebb])
                sc.op(sc.act, lambda e: e.activation(out=egc, in_=b1[:, 255:256], func=AF.Exp), reads=[b1b], writes=[egb])
                sc.op(sc.dve, lambda e, vaF=vaF, c=c: e.tensor_scalar(out=vaF, in0=VV[:, c * 256:(c + 1) * 256], scalar1=acol, scalar2=None, op0=ALU.mult),
                      reads=[vvb, acb], writes=[vab])
                sc.op(sc.dve, lambda e, abcF=abcF: e.tensor_scalar(out=abcF, in0=self.ones[:], scalar1=acol, scalar2=None, op0=ALU.mult),
                      reads=[self.onesb, acb], writes=[abcb])
                qk, qkb = self.banks[5], self.bankbufs[5]
                sc.op(sc.pe, lambda e, c=c: e.matmul(qk[:, 0:128], lhsT=KT[:, c * 128:(c + 1) * 128], rhs=QT[:, c * 128:(c + 1) * 128], start=True, stop=True),
                      reads=[ktb, qtb], writes=[qkb])
                sc.op(sc.dve, lambda e, mtF=mtF: e.tensor_tensor(out=mtF, in0=qk[:, 0:128], in1=self.tri[:], op=ALU.mult),
                      reads=[qkb, self.trib], writes=[mtb])
                nb_, nbb_ = self.banks[6], self.bankbufs[6]
                for j in range(2):
                    sc.op(sc.pe, lambda e, j=j, va=va, mt=mt: e.matmul(nb_[:, j * 128:(j + 1) * 128], lhsT=va[:, j * 128:(j + 1) * 128], rhs=mt, start=True, stop=False),
                          reads=[vab, mtb], writes=[nbb_])
                    sc.op(sc.pe, lambda e, j=j, c=c: e.matmul(nb_[:, j * 128:(j + 1) * 128], lhsT=CT[:, j * 128:(j + 1) * 128], rhs=QT[:, c * 128:(c + 1) * 128], start=False, stop=True),
                          reads=[ctb, qtb], writes=[nbb_])
                sc.op(sc.pe, lambda e, abc=abc, mt=mt: e.matmul(nb_[:, 256:384], lhsT=abc, rhs=mt, start=True, stop=False),
                      reads=[abcb, mtb], writes=[nbb_])
                sc.op(sc.pe, lambda e, c=c: e.matmul(nb_[:, 256:384], lhsT=NB, rhs=QT[:, c * 128:(c + 1) * 128], start=False, stop=True),
                      reads=[nbb, qtb], writes=[nbb_])
                sc.op(sc.dve, lambda e: e.tensor_tensor(out=DN, in0=nb_[:, 256:384], in1=EB, op=ALU.mult), reads=[nbb_, ebb], writes=[dnb])
                sc.op(sc.act, lambda e: e.activation(out=DN, in_=DN, func=AF.Abs), reads=[dnb], writes=[dnb])
                sc.op(sc.dve, lambda e: e.tensor_scalar(out=DN, in0=DN, scalar1=1.0, scalar2=None, op0=ALU.max), reads=[dnb], writes=[dnb])
                sc.op(sc.dve, lambda e: e.reciprocal(out=DN, in_=DN), reads=[dnb], writes=[dnb])
                sc.op(sc.dve, lambda e: e.tensor_tensor(out=WW, in0=DN, in1=EB, op=ALU.mult), reads=[dnb, ebb], writes=[wwb])
                for j in range(2):
                    sc.op(sc.dve, lambda e, j=j, c=c: e.tensor_tensor(out=YT[:, j * 2048 + c * 128: j * 2048 + (c + 1) * 128],
                                                                     in0=nb_[:, j * 128:(j + 1) * 128], in1=WW, op=ALU.mult),
                          reads=[nbb_, wwb], writes=[ytb[j]])
                if c % 4 == 3:
                    t0 = (c - 3) * 128
                    for j in range(2):
                        sc.op(sc.act, lambda e, j=j, t0=t0: e.activation(out=SQ4[:, j * 512:(j + 1) * 512], in_=YT[:, j * 2048 + t0: j * 2048 + t0 + 512],
                                                                        func=AF.Square), reads=[ytb[j]], writes=[sqb])
                    st_, stb_ = self.banks[7], self.bankbufs[7]
                    s2_, s2b_ = self.banks[2], self.bankbufs[2]
                    for j in range(2):
                        sc.op(sc.pe, lambda e, j=j, t0=t0: e.matmul(st_[:, 0:512], lhsT=self.o256[:], rhs=YT[:, j * 2048 + t0: j * 2048 + t0 + 512],
                                                                   start=(j == 0), stop=(j == 1)), reads=[self.o256b, ytb[j]], writes=[stb_])
                    for j in range(2):
                        sc.op(sc.pe, lambda e, j=j: e.matmul(s2_[:, 0:512], lhsT=self.o256[:], rhs=SQ4[:, j * 512:(j + 1) * 512],
                                                            start=(j == 0), stop=(j == 1)), reads=[self.o256b, sqb], writes=[s2b_])
                    sc.op(sc.dve, lambda e: e.tensor_copy(out=MU4, in_=st_[:, 0:512]), reads=[stb_], writes=[mub])
                    sc.op(sc.dve, lambda e: e.tensor_tensor(out=TM4, in0=MU4, in1=MU4, op=ALU.mult), reads=[mub], writes=[tmb])
                    sc.op(sc.dve, lambda e: e.tensor_tensor(out=TM4, in0=s2_[:, 0:512], in1=TM4, op=ALU.subtract), reads=[s2b_, tmb], writes=[tmb])
                    sc.op(sc.dve, lambda e: e.tensor_scalar(out=TM4, in0=TM4, scalar1=float(EPS), scalar2=None, op0=ALU.add), reads=[tmb], writes=[tmb])
                    sc.op(sc.act, lambda e: e.activation(out=TM4, in_=TM4, func=AF.Ln), reads=[tmb], writes=[tmb])
                    sc.op(sc.act, lambda e: e.activation(out=RS4, in_=TM4, func=AF.Exp, scale=-0.5), reads=[tmb], writes=[rsb])
                    for j in range(2):
                        yj = YT[:, j * 2048 + t0: j * 2048 + t0 + 512]
                        sc.op(sc.dve, lambda e, yj=yj: e.tensor_tensor(out=yj, in0=yj, in1=MU4, op=ALU.subtract), reads=[ytb[j], mub], writes=[ytb[j]])
                        sc.op(sc.dve, lambda e, yj=yj: e.tensor_tensor(out=yj, in0=yj, in1=RS4, op=ALU.mult), reads=[ytb[j], rsb], writes=[ytb[j]])
                        sc.op(sc.dve, lambda e, yj=yj, j=j, h=h, t0=t0: e.scalar_tensor_tensor(
                            out=yj, in0=yj, scalar=self.ngc[:, 2 * h + j:2 * h + j + 1],
                            in1=ODT[:, j * 2048 + t0: j * 2048 + t0 + 512], op0=ALU.mult, op1=ALU.mult),
                            reads=[ytb[j], self.ngcb, odb[j]], writes=[ytb[j]])
                ub, ubb = self.banks[3], self.bankbufs[3]
                sc.op(sc.pe, lambda e, c=c, va=va: e.matmul(ub[:, 0:256], lhsT=KK[:, c * 128:(c + 1) * 128], rhs=va, start=True, stop=True),
                      reads=[kkb, vab], writes=[ubb])
                sc.op(sc.pe, lambda e, c=c, abc=abc: e.matmul(ub[:, 256:384], lhsT=KK[:, c * 128:(c + 1) * 128], rhs=abc, start=True, stop=True),
                      reads=[kkb, abcb], writes=[ubb])
                sc.op(sc.dve, lambda e: e.tensor_tensor(out=CTF, in0=ub[:, 0:256], in1=CTF, op=ALU.add), reads=[ubb, ctb], writes=[ctb])
                sc.op(sc.dve, lambda e: e.tensor_scalar(out=CTF, in0=CTF, scalar1=egc, scalar2=None, op0=ALU.mult), reads=[ctb, egb], writes=[ctb])
                sc.op(sc.dve, lambda e: e.tensor_tensor(out=NBF, in0=ub[:, 256:384], in1=NBF, op=ALU.add), reads=[ubb, nbb], writes=[nbb])
                sc.op(sc.dve, lambda e: e.tensor_scalar(out=NBF, in0=NBF, scalar1=egc, scalar2=None, op0=ALU.mult), reads=[nbb, egb], writes=[nbb])
            for j in range(2):
                sc.op(sc.pool, lambda e, h=h, j=j: e.dma_start(
                    out=self.MIXT.ap[(16 + 2 * h + j) * 128:(17 + 2 * h + j) * 128, :], in_=YT[:, j * 2048:(j + 1) * 2048]),
                    reads=[ytb[j]], writes=[self.db("MIXT", 16 + 2 * h + j)])

    def phase_proj_ln(self, l, j, final=False):
        sc = self.sc
        self.wqs = None
        T = 512
        if j == 0:
            parts, wl, act, res, dst = [(0, 16), (16, 16)], self.wl_out[l], self.MIXT, self.XT, self.X1T
        else:
            parts, wl, act, res, dst = [(0, 15), (15, 15), (30, 14), (44, 14), (58, 14), (72, 14)], self.wl_d[l], self.HT, self.X1T, self.XT
        NTT = S // T
        KM = max(p[1] for p in parts)
        av = [self.carve2("plA%d" % i, i * KM * T, KM * T) for i in range(2)]
        o2 = 2 * KM * T
        resF = self.carve("plR", o2, 32 * T, F32)
        abufs = [[Buf() for _ in range(KM)] for _ in range(2)]
        rbufs = [Buf() for _ in range(32)]
        gcol = self.lnp[:, (l * 2 + j) * 64:(l * 2 + j) * 64 + 32]
        bcol = self.lnp[:, (l * 2 + j) * 64 + 32:(l * 2 + j) * 64 + 64]
        if "lnt" not in self.rots:
            self.mkrot("lnt", 4, [128, 512])
        self.bank_groups["aux2"] = [6, 7]
        (mean, meanb), (m2, m2b), (rstd, rstdb), (nmr, nmrb) = [self.rots["lnt"][i] for i in range(4)]
        tmp, tmpb = nmr, nmrb
        mbank, mbb = self.banks[4], self.bankbufs[4]
        qbank, qbb = self.banks[5], self.bankbufs[5]

        def make_epi(tt):
            def epi(dc):
                z = resF[:, dc * T:(dc + 1) * T]
                stg, sb_ = self.rot("stage")
                sc.op(sc.dve, lambda e: e.tensor_tensor(out=z, in0=z, in1=rstd[:, 0:T], op=ALU.mult),
                      reads=[rbufs[dc], rstdb], writes=[rbufs[dc]])
                sc.op(sc.dve, lambda e: e.tensor_tensor(out=z, in0=z, in1=nmr[:, 0:T], op=ALU.add),
                      reads=[rbufs[dc], nmrb], writes=[rbufs[dc]])
                sc.op(sc.dve, lambda e: e.tensor_scalar(
                    out=stg[:, 0:T], in0=z, scalar1=gcol[:, dc:dc + 1], scalar2=bcol[:, dc:dc + 1], op0=ALU.mult, op1=ALU.add),
                    reads=[rbufs[dc], self.lnpb], writes=[sb_])
                if not final:
                    sc.op(sc.pool, lambda e: e.dma_start(
                        out=dst.ap[dc * 128:(dc + 1) * 128, tt * T:(tt + 1) * T], in_=stg[:, 0:T]),
                        reads=[sb_], writes=[self.db(dst.name, dc)])
                else:
                    bank, bb = self.next_bank("aux2")
                    nts = T // 128
                    for ts in range(nts):
                        sc.op(sc.pe, lambda e, ts=ts: e.transpose(
                            out=bank[:, ts * 128:(ts + 1) * 128], in_=stg[:, ts * 128:(ts + 1) * 128], identity=self.ident[:]),
                            reads=[sb_, self.identb], writes=[bb])
                    o2s, o2b = self.rot("stage")
                    self.evac(bank, bb, T, o2s[:, 0:T], o2b)
                    ov = self.out[tt * T:(tt + 1) * T, dc * 128:(dc + 1) * 128].rearrange("(s p) d -> p s d", p=128)
                    sc.op(sc.pool, lambda e: e.dma_start(
                        out=ov, in_=o2s[:, 0:T].rearrange("p (s d) -> p s d", d=128)),
                        reads=[o2b], writes=[self.db("out", dc)])
            return epi

        pending = [None]
        issued = [0]
        pc = 0
        for tt in range(NTT):
            issued[0] = 0
            for pi, (kb, kn) in enumerate(parts):
                bi = pc % 2
                pc += 1
                actvF, actv = av[bi]
                self.load_actT(act, kb * 128, kn, tt * T, T, actvF, abufs[bi])
                rhs_fn = lambda kc, hh, actv=actv, bi=bi: (actv[:, kc * T:(kc + 1) * T], abufs[bi][kc])
                firstp = (pi == 0)
                lastp = (pi == len(parts) - 1)

                def consumer(tag, dc, bks, firstp=firstp, lastp=lastp, tt=tt):
                    bank, bb = bks[0]
                    z = resF[:, dc * T:(dc + 1) * T]
                    if firstp:
                        while issued[0] <= min(dc + 2, 31):
                            k = issued[0]
                            issued[0] += 1
                            if pending[0] is not None:
                                pending[0](k)
                            sv = res.ap[k * 128:(k + 1) * 128, tt * T:(tt + 1) * T]
                            zk = resF[:, k * T:(k + 1) * T]
                            sc.op(sc.pool, lambda e, zk=zk, sv=sv: e.dma_start(out=zk, in_=sv), reads=[self.db(res.name, k)], writes=[rbufs[k]])
                        sc.op(sc.dve, lambda e: e.scalar_tensor_tensor(out=z, in0=z, scalar=float(ALPHA), in1=bank[:, 0:T],
                                                                        op0=ALU.mult, op1=ALU.add),
                              reads=[bb, rbufs[dc]], writes=[rbufs[dc]])
                    else:
                        sc.op(sc.dve, lambda e: e.tensor_tensor(out=z, in0=z, in1=bank[:, 0:T], op=ALU.add),
                              reads=[bb, rbufs[dc]], writes=[rbufs[dc]])
                    if lastp:
                        sq, sqb = self.rot("stage")
                        sc.op(sc.act, lambda e: e.activation(out=sq[:, 0:T], in_=z, func=AF.Square), reads=[rbufs[dc]], writes=[sqb])
                        sc.op(sc.pe, lambda e: e.matmul(mbank[:, 0:T], lhsT=self.onesD[:], rhs=z, start=(dc == 0), stop=(dc == 31)),
                              reads=[self.onesDb, rbufs[dc]], writes=[mbb])
                        sc.op(sc.pe, lambda e: e.matmul(qbank[:, 0:T], lhsT=self.onesD[:], rhs=sq[:, 0:T], start=(dc == 0), stop=(dc == 31)),
                              reads=[self.onesDb, sqb], writes=[qbb])

                self.proj([(wl, dc, "p") for dc in range(32)], kn, rhs_fn, T, consumer, kbase=kb)
            sc.op(sc.dve, lambda e: e.tensor_copy(out=mean[:, 0:T], in_=mbank[:, 0:T]), reads=[mbb], writes=[meanb])
            sc.op(sc.dve, lambda e: e.tensor_tensor(out=m2[:, 0:T], in0=mean[:, 0:T], in1=mean[:, 0:T], op=ALU.mult),
                  reads=[meanb], writes=[m2b])
            sc.op(sc.dve, lambda e: e.tensor_tensor(out=m2[:, 0:T], in0=qbank[:, 0:T], in1=m2[:, 0:T], op=ALU.subtract),
                  reads=[qbb, m2b], writes=[m2b])
            sc.op(sc.dve, lambda e: e.tensor_scalar(out=m2[:, 0:T], in0=m2[:, 0:T], scalar1=float(EPS), scalar2=None, op0=ALU.add),
                  reads=[m2b], writes=[m2b])
            sc.op(sc.act, lambda e: e.activation(out=tmp[:, 0:T], in_=m2[:, 0:T], func=AF.Ln), reads=[m2b], writes=[tmpb])
            sc.op(sc.act, lambda e: e.activation(out=rstd[:, 0:T], in_=tmp[:, 0:T], func=AF.Exp, scale=-0.5),
                  reads=[tmpb], writes=[rstdb])
            sc.op(sc.dve, lambda e: e.scalar_tensor_tensor(out=nmr[:, 0:T], in0=mean[:, 0:T], scalar=-1.0, in1=rstd[:, 0:T],
                                                            op0=ALU.mult, op1=ALU.mult), reads=[meanb, rstdb], writes=[nmrb])
            pending[0] = make_epi(tt)
        for dc in range(32):
            pending[0](dc)

    def phase_ffn_upgate(self, l):
        sc = self.sc
        self.wqs = None
        self.nw = 2
        T = 1024
        abufs = [Buf() for _ in range(32)]
        X1F, X1 = self.carve2("ffx", 0, 32768)
        AFv = self.carve("fff", 42240, 4096, F32)
        halo = AFv[:, 0:2 * NFF]
        halob = Buf()
        gbufs = [(AFv[:, 256 + i * 520: 256 + i * 520 + 514], Buf()) for i in range(2)]
        accs = [(AFv[:, 1400 + i * 512: 1400 + (i + 1) * 512], Buf()) for i in range(2)]
        sil = [(AFv[:, 2500 + i * 512: 2500 + (i + 1) * 512], Buf()) for i in range(2)]
        sc.op(sc.dve, lambda e: e.memset(halo, 0.0), writes=[halob])
        cw = self.ffc
        rr = [0]
        state = {}
        for tt in range(S // T):
            self.load_actT(self.X1T, 0, 32, tt * T, T, X1F, abufs)
            rhs_fn = lambda kc, hh: (X1[:, kc * T + hh * 512: kc * T + (hh + 1) * 512], abufs[kc])

            def consumer(tag, fc, bks, tt=tt):
                c0 = (l * NFF + fc) * 4
                if tag == "g":
                    state["cur"] = []
                    for hh, (bank, bb) in enumerate(bks):
                        i = rr[0] % 2
                        rr[0] += 1
                        gb, gbb = gbufs[i]
                        acc, accb = accs[i]
                        sl, slb = sil[i]
                        state["cur"].append((sl, slb))
                        sc.op(sc.act, lambda e, gb=gb: e.activation(out=gb[:, 0:2], in_=halo[:, fc * 2:fc * 2 + 2], func=AF.Copy),
                              reads=[halob], writes=[gbb])
                        sc.op(sc.dve, lambda e, gb=gb, bank=bank: e.tensor_copy(out=gb[:, 2:514], in_=bank[:, 0:512]), reads=[bb], writes=[gbb])
                        sc.op(sc.act, lambda e, gb=gb: e.activation(out=halo[:, fc * 2:fc * 2 + 2], in_=gb[:, 512:514], func=AF.Copy),
                              reads=[gbb], writes=[halob])
                        sc.op(sc.dve, lambda e, gb=gb, acc=acc: e.tensor_scalar(out=acc, in0=gb[:, 2:514], scalar1=cw[:, c0 + 2:c0 + 3],
                                                                              scalar2=cw[:, c0 + 3:c0 + 4], op0=ALU.mult, op1=ALU.add),
                              reads=[gbb, self.ffcb], writes=[accb])
                        sc.op(sc.dve, lambda e, gb=gb, acc=acc: e.scalar_tensor_tensor(out=acc, in0=gb[:, 1:513], scalar=cw[:, c0 + 1:c0 + 2], in1=acc,
                                                                                      op0=ALU.mult, op1=ALU.add),
                              reads=[gbb, self.ffcb, accb], writes=[accb])
                        sc.op(sc.dve, lambda e, gb=gb, acc=acc: e.scalar_tensor_tensor(out=acc, in0=gb[:, 0:512], scalar=cw[:, c0:c0 + 1], in1=acc,
                                                                                      op0=ALU.mult, op1=ALU.add),
                              reads=[gbb, self.ffcb, accb], writes=[accb])
                        sc.op(sc.act, lambda e, sl=sl, acc=acc: e.activation(out=sl, in_=acc, func=AF.Silu), reads=[accb], writes=[slb])
                else:
                    for hh, (bank, bb) in enumerate(bks):
                        sl, slb = state["cur"][hh]
                        stg, sb_ = self.rot("stage")
                        sc.op(sc.dve, lambda e, stg=stg, bank=bank, sl=sl: e.tensor_tensor(out=stg[:], in0=bank[:, 0:512], in1=sl, op=ALU.mult),
                              reads=[bb, slb], writes=[sb_])
                        sc.op(sc.pool, lambda e, stg=stg, hh=hh: e.dma_start(
                            out=self.HT.ap[fc * 128:(fc + 1) * 128, tt * T + hh * 512: tt * T + (hh + 1) * 512], in_=stg[:]),
                            reads=[sb_], writes=[self.db("HT", fc)])

            plan = []
            for fc in range(NFF):
                plan.append((self.wl_g[l], fc, "g"))
                plan.append((self.wl_u[l], fc, "u"))
            self.proj(plan, 32, rhs_fn, T, consumer)
        self.nw = 3


def _wl(W):
    K, F = W.shape
    KC, FC = K // 128, F // 128
    return np.ascontiguousarray(W.reshape(KC, 128, FC, 128).transpose(2, 1, 0, 3)).reshape(FC, 128, KC * 128)


def _consts():
    c = {}
    c["ident"] = np.eye(128, dtype=np.float32)
    c["identr"] = np.eye(128, dtype=np.float32)
    c["onesr"] = np.ones((128, 128), np.float32)
    c["onesD"] = np.full((128, 128), 1.0 / D, np.float32)
    sl = np.arange(128)[:, None]
    tl = np.arange(512)[None, :]
    lma = np.zeros((128, 16, 512), np.float32)
    for dd in range(16):
        dist = 128 * (dd - 3) + tl - sl
        cnt = ((dist >= 0) & (dist <= 128)).astype(np.float64) \
            + ((dist >= 0) & (dist % 4 == 0) & (dist <= 512)) + ((dist >= 0) & (dist % 16 == 0) & (dist <= 2048))
        lma[:, dd, :] = np.where(cnt > 0, np.log(np.maximum(cnt, 1.0)), NEG)
    c["lma"] = lma.reshape(128, 16 * 512)
    lmb = np.zeros((128, 4, 512), np.float32)
    for j in range(4):
        lmb[:, j, :] = np.where(sl + 128 * j <= tl, 0.0, NEG)
    c["lmb"] = lmb.reshape(128, 4 * 512)
    en = np.zeros((16, 8, 8, 128), np.float32)
    for n in range(8):
        en[:, n, n, :] = -NEG
    c["en"] = en.reshape(128, 1024)
    rm = np.zeros((16, 8, 16), np.float32)
    for i in range(16):
        rm[i, :, i] = 1.0
    c["rm"] = rm.reshape(128, 16)
    vm = np.zeros((128, 16, 8), np.float32)
    own = np.zeros((128, 16, 8), np.float32)
    for i in range(16):
        for n in range(8):
            if n >= i // 2:
                vm[:, i, n] = -1e30
            if n == i // 2:
                own[:, i, n] = 1.0
    lmc = np.zeros((128, 4, 512), np.float32)
    for j in range(4):
        lmc[:, j, :] = np.where(sl + 128 * j < tl, 0.0, NEG)
    c["lmc"] = lmc.reshape(128, 4 * 512)
    jj = np.arange(128)
    c["tri"] = (jj[:, None] <= jj[None, :]).astype(np.float32)
    c["trige"] = (jj[:, None] >= jj[None, :]).astype(np.float32)
    c["ones"] = np.ones((128, 128), np.float32)
    c["o256"] = np.full((128, 128), 1.0 / 256, np.float32)
    c["vm"] = vm.reshape(128, 128)
    c["own"] = own.reshape(128, 128)
    return c


def _prep_shared(inp):
    sh = _consts()
    sh["wl_in0"] = _wl(inp["w_in_ab"][0])
    sh["wl_out0"] = _wl(inp["w_out_ab"][0])
    wcd = np.zeros((D, 97 * 128), np.float32)
    wcd[:, :12304] = inp["w_in_cd"][0]
    sh["wl_in1"] = _wl(wcd)
    sh["wl_out1"] = _wl(inp["w_out_cd"][0])
    for l in range(2):
        sh["wl_g%d" % l] = _wl(inp["ffn_w_gate"][l])
        sh["wl_u%d" % l] = _wl(inp["ffn_w_up"][l])
        sh["wl_d%d" % l] = _wl(inp["ffn_w_down"][l])
    lnp = np.zeros((128, 256), np.float32)
    for l in range(2):
        for j in range(2):
            o = (l * 2 + j) * 64
            lnp[:, o:o + 32] = inp["ln_g"][l, j].reshape(32, 128).T
            lnp[:, o + 32:o + 64] = inp["ln_b"][l, j].reshape(32, 128).T
    sh["lnp"] = lnp
    ffc = np.zeros((128, 2, NFF, 4), np.float32)
    for l in range(2):
        for k in range(3):
            ffc[:, l, :, k] = inp["ffn_conv"][l, k].reshape(NFF, 128).T
        ffc[:, l, :, 3] = inp["ffn_conv_b"][l].reshape(NFF, 128).T
    sh["ffc"] = ffc.reshape(128, 2 * NFF * 4)
    cdc = np.zeros((128, 16, 5), np.float32)
    for k in range(4):
        cdc[:, :, k] = inp["conv_cd"][0, k].reshape(16, 128).T
    cdc[:, :, 4] = inp["conv_cd_b"][0].reshape(16, 128).T
    sh["cdc"] = cdc.reshape(128, 80)
    sh["ngc"] = np.ascontiguousarray(inp["norm_cd_g"][0].reshape(16, 128).T)
    bif = np.zeros((128, 1), np.float32)
    bif[:16, 0] = inp["b_if_cd"][0]
    sh["bif"] = bif
    return sh


def run(inp, cores=8, dump=(), stop_after=None, trace=False):
    b = Builder(dump=dump, stop_after=stop_after)
    nc = b.build()
    sh = _prep_shared(inp)
    in_maps = []
    for c in range(cores):
        m = {k: sh[k] for k in b.in_names if k in sh}
        m["x"] = np.ascontiguousarray(inp["x"][c])
        in_maps.append(m)
    res = run_bass_kernel_spmd(nc, in_maps, core_ids=list(range(cores)), trace=trace)
    return res


def kernel(**inputs):
    inp = {k: np.asarray(v) for k, v in inputs.items()}
    res = run(inp, cores=8)
    return np.stack([r["out"] for r in res.results], axis=0)
```

```python
import numpy as np
from contextlib import ExitStack
import concourse.bass as bass
import concourse.mybir as mybir
from concourse.bass_utils import run_bass_kernel_spmd

F32 = mybir.dt.float32
F32R = mybir.dt.float32r
AF = mybir.ActivationFunctionType
ALU = mybir.AluOpType
AX = mybir.AxisListType

S = 2048
D = 4096
DFF = 11008
NFF = 86
ALPHA = 4.0 ** 0.25
EPS = 1e-5
QSCALE = 128.0 ** -0.5
NEG = -200.0
PRJ_ROWS = 97 * 128


class Buf:
    __slots__ = ("w", "r")

    def __init__(self):
        self.w = None
        self.r = {}


class Q:
    def __init__(self, name, sem, is_dma=False, skip_self=False):
        self.name = name
        self.sem = sem
        self.is_dma = is_dma
        self.skip_self = skip_self
        self.count = 0
        self.seen = {}
        self.prog = []


class Sched:
    NP = 40

    def __init__(self, nc, stack):
        mk = lambda n: stack.enter_context(nc.semaphore(n))
        self.pe = Q("pe", mk("s_pe"), skip_self=True)
        self.dve = Q("dve", mk("s_dve"))
        self.act = Q("act", mk("s_act"))
        self.sp = Q("sp", None, is_dma=True)
        self.pool = Q("pool", None, is_dma=True)
        self.queues = [self.pe, self.dve, self.act, self.sp, self.pool]
        self.dsem = [mk("s_d%d" % j) for j in range(self.NP)]
        self.dval = [0] * self.NP
        self.drr = 0

    def op(self, q, emit, reads=(), writes=()):
        deps = {}

        def add(ev):
            if ev is None:
                return
            k, sem, v = ev
            o = deps.get(k)
            if o is None or o[1] < v:
                deps[k] = (sem, v)

        for b in reads:
            add(b.w)
        for b in writes:
            add(b.w)
            for ev in b.r.values():
                add(ev)
        if q.is_dma:
            j = self.drr
            self.drr = (j + 1) % self.NP
            if self.dval[j] > 0:
                add((("d", j), self.dsem[j], self.dval[j]))
            self.dval[j] += 16
            ev = (("d", j), self.dsem[j], self.dval[j])
            inc = 16
        else:
            q.count += 1
            ev = (q.name, q.sem, q.count)
            inc = 1
        for k, (sem, v) in deps.items():
            if k == q.name and q.skip_self:
                continue
            if q.seen.get(k, 0) >= v:
                continue
            q.seen[k] = v
            q.prog.append(lambda e, sem=sem, v=v: e.wait_ge(sem, v))
        sem_ = ev[1]
        q.prog.append(lambda e, emit=emit, sem_=sem_, inc=inc: emit(e).then_inc(sem_, inc))
        for b in reads:
            o = b.r.get(ev[0])
            if o is None or o[2] < ev[2]:
                b.r[ev[0]] = ev
        for b in writes:
            b.w = ev
            b.r = {}
        return ev

    def barrier(self):
        for q in self.queues:
            for q2 in (self.pe, self.dve, self.act):
                if q2 is q:
                    continue
                if q2.count > q.seen.get(q2.name, 0):
                    q.seen[q2.name] = q2.count
                    q.prog.append(lambda e, sem=q2.sem, v=q2.count: e.wait_ge(sem, v))
            for j in range(self.NP):
                v = self.dval[j]
                if v > q.seen.get(("d", j), 0):
                    q.seen[("d", j)] = v
                    q.prog.append(lambda e, sem=self.dsem[j], v=v: e.wait_ge(sem, v))


class DT:
    def __init__(self, name, ap):
        self.name = name
        self.ap = ap


class Builder:
    def __init__(self, dump=(), stop_after=None):
        self.dump = set(dump)
        self.stop_after = stop_after
        self.nc = bass.Bass("TRN2", target_bir_lowering=False)
        self.stack = ExitStack()
        self.dbufs = {}
        self.bank_rr = {}
        self.rot_rr = {}
        self.rots = {}
        self.w_rr = 0
        self.nw = 3
        self.wqs = None
        self.ev_rr = 0
        self.in_names = []

    def db(self, name, idx=0):
        k = (name, idx)
        b = self.dbufs.get(k)
        if b is None:
            b = self.dbufs[k] = Buf()
        return b

    def dram_in(self, name, shape):
        self.in_names.append(name)
        return self.nc.dram_tensor(name, list(shape), F32, kind="ExternalInput").ap()

    def dram_tmp(self, name, shape):
        kind = "ExternalOutput" if name in self.dump else "Internal"
        return DT(name, self.nc.dram_tensor(name, list(shape), F32, kind=kind).ap())

    def sb(self, name, shape, dt=F32):
        return self.stack.enter_context(self.nc.sbuf_tensor(name, list(shape), dt))

    def carve(self, name, off, n, dt=F32, parts=128):
        self.carve_id = getattr(self, "carve_id", 0) + 1
        t = self.nc.alloc_sbuf_tensor_at("%s_c%d" % (name, self.carve_id), [parts, n], dt, offset=self.arena_addr + off * 4)
        return t.ap() if hasattr(t, "ap") and callable(t.ap) else t[:]

    def carve2(self, name, off, n, parts=128):
        return self.carve(name + "F", off, n, F32, parts), self.carve(name + "R", off, n, F32R, parts)

    def next_bank(self, grp):
        lst = self.bank_groups[grp]
        i = self.bank_rr.get(grp, 0)
        self.bank_rr[grp] = (i + 1) % len(lst)
        b = lst[i]
        return self.banks[b], self.bankbufs[b]

    def rot(self, name):
        lst = self.rots[name]
        i = self.rot_rr.get(name, 0)
        self.rot_rr[name] = (i + 1) % len(lst)
        return lst[i]

    def mkrot(self, name, n, shape, dt=F32):
        self.rots[name] = [(self.sb("%s%d" % (name, i), shape, dt), Buf()) for i in range(n)]

    def const_load(self, name, shape, dt=F32):
        src = self.dram_in(name, shape)
        t = self.sb("c_" + name, shape, dt)
        b = Buf()
        s = src.bitcast(F32R) if dt == F32R else src
        self.sc.op(self.sc.pool, lambda e, t=t, s=s: e.dma_start(out=t[:], in_=s), writes=[b])
        return t, b

    def const_into(self, name, shape, dstF, dstR):
        src = self.dram_in(name, shape)
        b = Buf()
        self.sc.op(self.sc.pool, lambda e: e.dma_start(out=dstF, in_=src), writes=[b])
        return dstR, b

    def wload(self, src, n):
        i = self.w_rr % self.nw
        self.w_rr += 1
        slot = self.wslots[i]
        slotF = self.wslotsF[i]
        wqs = self.wqs or [self.sc.sp]
        q = wqs[self.w_rr % len(wqs)]
        self.sc.op(q, lambda e, slotF=slotF, src=src, n=n: e.dma_start(out=slotF[:, 0:n], in_=src),
                   writes=[self.wbufs[i]])
        return slot, self.wbufs[i]

    def proj(self, plan, KC, rhs_fn, N, consumer, grp="proj", kbase=0):
        sc = self.sc
        NH = (N + 511) // 512
        Nh = N // NH
        for (wl, fc, tag) in plan:
            bks = [self.next_bank(grp) for _ in range(NH)]
            k0 = 0
            while k0 < KC:
                n = min(32, KC - k0)
                slot, wbuf = self.wload(wl[fc, :, (kbase + k0) * 128:(kbase + k0 + n) * 128], n * 128)
                for kk in range(n):
                    kc = k0 + kk
                    for hh in range(NH):
                        bank, bbuf = bks[hh]
                        rap, rbuf = rhs_fn(kc, hh)
                        sc.op(sc.pe, lambda e, bank=bank, slot=slot, kk=kk, rap=rap, st=(kc == 0), sp=(kc == KC - 1):
                              e.matmul(bank[:, 0:Nh], lhsT=slot[:, kk * 128:(kk + 1) * 128], rhs=rap, start=st, stop=sp),
                              reads=[wbuf, rbuf], writes=[bbuf])
                k0 += n
            consumer(tag, fc, bks)

    def load_actT(self, src, row0, KC, t0, T, dst, dbufs, q=None):
        sc = self.sc
        q = q or sc.sp
        for kc in range(KC):
            sv = src.ap[row0 + kc * 128: row0 + (kc + 1) * 128, t0:t0 + T]
            dv = dst[:, kc * T:(kc + 1) * T]
            sc.op(q, lambda e, sv=sv, dv=dv: e.dma_start(out=dv, in_=sv),
                  reads=[self.db(src.name, row0 // 128 + kc)], writes=[dbufs[kc]])

    def evac(self, bank, bbuf, N, dst, dstbuf, scale=None, func=None, bias=None, eng=None, extra_reads=()):
        sc = self.sc
        if eng is None:
            if func is not None or bias is not None:
                eng = "act"
            else:
                self.ev_rr ^= 1
                eng = "act" if self.ev_rr else "dve"
        src = bank[:, 0:N]
        if eng == "dve":
            if scale is None:
                sc.op(sc.dve, lambda e: e.tensor_copy(out=dst, in_=src), reads=[bbuf] + list(extra_reads), writes=[dstbuf])
            else:
                sc.op(sc.dve, lambda e: e.tensor_scalar(out=dst, in0=src, scalar1=float(scale), scalar2=None, op0=ALU.mult),
                      reads=[bbuf] + list(extra_reads), writes=[dstbuf])
        else:
            kw = {}
            if scale is not None:
                kw["scale"] = float(scale)
            if bias is not None:
                kw["bias"] = bias
            f = func if func is not None else AF.Copy
            if bias is not None and func is None:
                f = AF.Identity
            sc.op(sc.act, lambda e: e.activation(out=dst, in_=src, func=f, **kw), reads=[bbuf] + list(extra_reads), writes=[dstbuf])

    def build(self):
        nc = self.nc
        st = self.stack
        self.sc = sc = Sched(nc, st)
        self.x_in = self.dram_in("x", [S, D])
        self.out = nc.dram_tensor("out", [S, D], F32, kind="ExternalOutput").ap()
        self.wl_in = [self.dram_in("wl_in0", [96, 128, 4096]), self.dram_in("wl_in1", [97, 128, 4096])]
        self.wl_out = [self.dram_in("wl_out0", [32, 128, 4096]), self.dram_in("wl_out1", [32, 128, 4096])]
        self.wl_g = [self.dram_in("wl_g%d" % l, [NFF, 128, 4096]) for l in range(2)]
        self.wl_u = [self.dram_in("wl_u%d" % l, [NFF, 128, 4096]) for l in range(2)]
        self.wl_d = [self.dram_in("wl_d%d" % l, [32, 128, NFF * 128]) for l in range(2)]
        self.XT = self.dram_tmp("XT", [D, S])
        self.PRJ = self.dram_tmp("PRJ", [PRJ_ROWS, S])
        self.MIXT = self.dram_tmp("MIXT", [D, S])
        self.X1T = self.dram_tmp("X1T", [D, S])
        self.HT = self.dram_tmp("HT", [DFF, S])

        AREN = 46336
        self.arena_t = self.sb("arena", [128, AREN], F32)
        self.arena_addr = None
        for al in nc.allocations:
            if getattr(al, "name", None) == "arena_set":
                self.arena_addr = al.memorylocations[0].addr
        assert self.arena_addr is not None
        ws = [self.carve2("wslot%d" % i, 34048 + i * 4096, 4096) for i in range(3)]
        self.wslotsF = [w[0] for w in ws]
        self.wslots = [w[1] for w in ws]
        self.wbufs = [Buf() for _ in range(3)]
        self.banks = [st.enter_context(nc.psum_tensor("ps%d" % i, [128, 512], F32)) for i in range(8)]
        self.bankbufs = [Buf() for _ in range(8)]
        self.bank_groups = {"proj": [0, 1, 2, 3], "aux": [4, 5, 6, 7]}
        self.mkrot("stage", 4, [128, 512])
        cF, cR = self.carve2("cst", 32768, 1280)
        self.cstF = cF
        self.ident, self.identb = self.const_load("ident", [128, 128])
        self.identr, self.identrb = self.const_into("identr", [128, 128], cF[:, 0:128], cR[:, 0:128])
        self.onesr, self.onesrb = self.const_into("onesr", [128, 128], cF[:, 128:256], cR[:, 128:256])
        self.onesD, self.onesDb = self.const_load("onesD", [128, 128])
        self.lnp, self.lnpb = self.const_load("lnp", [128, 256])
        self.ffc, self.ffcb = self.const_load("ffc", [128, 2 * NFF * 4])
        self.vm, self.vmb = self.const_load("vm", [128, 128])
        self.own, self.ownb = self.const_load("own", [128, 128])
        self.en, self.enb = self.const_into("en", [128, 1024], cF[:, 256:1280], cR[:, 256:1280])
        self.rm, self.rmb = self.const_load("rm", [128, 16])
        self.lma_d = self.dram_in("lma", [128, 16 * 512])
        self.lmb_d = self.dram_in("lmb", [128, 4 * 512])

        self.tri, self.trib = self.const_load("tri", [128, 128])
        self.ones, self.onesb = self.const_load("ones", [128, 128])
        self.o256, self.o256b = self.const_load("o256", [128, 128])
        self.cdc, self.cdcb = self.const_load("cdc", [128, 80])
        self.ngc, self.ngcb = self.const_load("ngc", [128, 16])
        self.bif, self.bifb = self.const_load("bif", [128, 1])
        self.lmc_d = self.dram_in("lmc", [128, 4 * 512])
        self.trige_d = self.dram_in("trige", [128, 128])
        l0only = self.stop_after in ("P0", "P1", "P2", "P3a", "P3b", "P3c")
        phases = [("P0", self.phase_transpose_in),
                  ("P1", lambda: self.phase_in_proj(0)),
                  ("P2", self.phase_attn_ab),
                  ("P3a", lambda: self.phase_proj_ln(0, 0)),
                  ("P3b", lambda: self.phase_ffn_upgate(0)),
                  ("P3c", lambda: self.phase_proj_ln(0, 1, final=l0only)),
                  ("Q1", lambda: self.phase_in_proj(1)),
                  ("Q2c", self.phase_attn_c),
                  ("Q2d", self.phase_mlstm),
                  ("Q3a", lambda: self.phase_proj_ln(1, 0)),
                  ("Q3b", lambda: self.phase_ffn_upgate(1)),
                  ("Q3c", lambda: self.phase_proj_ln(1, 1, final=True)),
                  ]
        for name, fn in phases:
            fn()
            sc.barrier()
            if self.stop_after == name:
                break
        if self.stop_after is not None and self.stop_after not in ("P3c", "Q3c"):
            stg, sb_ = self.rot("stage")
            sc.op(sc.dve, lambda e: e.memset(stg[:], 0.0), writes=[sb_])
            sc.op(sc.pool, lambda e: e.dma_start(out=self.out[0:128, 0:512], in_=stg[:]), reads=[sb_], writes=[self.db("out", 0)])
            sc.barrier()
        with nc.Block() as block:
            @block.tensor
            def _(e):
                for f in sc.pe.prog:
                    f(e)

            @block.vector
            def _(e):
                for f in sc.dve.prog:
                    f(e)

            @block.scalar
            def _(e):
                for f in sc.act.prog:
                    f(e)

            @block.gpsimd
            def _(e):
                for f in sc.pool.prog:
                    f(e)

            @block.sync
            def _(e):
                for f in sc.sp.prog:
                    f(e)
        return nc

    def phase_transpose_in(self):
        sc = self.sc
        AFv = self.carve("p0x", 0, 32768, F32)
        abufs = [[Buf() for _ in range(4)] for _ in range(2)]
        for tt in range(4):
            half = tt % 2
            base = half * 16384
            for ts in range(4):
                r0 = (tt * 4 + ts) * 128
                sc.op(sc.sp, lambda e, base=base, ts=ts, r0=r0: e.dma_start(
                    out=AFv[:, base + ts * 4096: base + (ts + 1) * 4096], in_=self.x_in[r0:r0 + 128, :]),
                    writes=[abufs[half][ts]])
            for kc in range(32):
                bank, bb = self.next_bank("aux")
                for ts in range(4):
                    o = base + ts * 4096 + kc * 128
                    sc.op(sc.pe, lambda e, bank=bank, ts=ts, o=o: e.transpose(
                        out=bank[:, ts * 128:(ts + 1) * 128], in_=AFv[:, o:o + 128], identity=self.ident[:]),
                        reads=[abufs[half][ts], self.identb], writes=[bb])
                stg, sb_ = self.rot("stage")
                self.evac(bank, bb, 512, stg[:], sb_)
                sc.op(sc.pool, lambda e, stg=stg, kc=kc, tt=tt: e.dma_start(
                    out=self.XT.ap[kc * 128:(kc + 1) * 128, tt * 512:(tt + 1) * 512], in_=stg[:]),
                    reads=[sb_], writes=[self.db("XT", kc)])

    def phase_in_proj(self, l):
        sc = self.sc
        self.wqs = None
        self.nw = 2
        wl = self.wl_in[l]
        FC = 96 if l == 0 else 97
        T = 1024
        abufs = [Buf() for _ in range(32)]
        xF, xR = self.carve2("p1x", 0, 32768)
        if l == 1:
            F1 = self.carve("p1f", 42240, 4096, F32)
            halo = F1[:, 0:48]
            halob = Buf()
            gbufs = [(F1[:, 64 + i * 520: 64 + i * 520 + 515], Buf()) for i in range(2)]
            accs = [(F1[:, 1200 + i * 512: 1200 + (i + 1) * 512], Buf()) for i in range(2)]
            sc.op(sc.dve, lambda e: e.memset(halo, 0.0), writes=[halob])
            rr = [0]
        for tt in range(S // T):
            self.load_actT(self.XT, 0, 32, tt * T, T, xF, abufs)
            rhs_fn = lambda kc, hh: (xR[:, kc * T + hh * 512: kc * T + (hh + 1) * 512], abufs[kc])

            def consumer(tag, fc, bks, tt=tt):
                for hh, (bank, bb) in enumerate(bks):
                    stg, sb_ = self.rot("stage")
                    if l == 0:
                        scale = QSCALE if (fc < 16 or 48 <= fc < 64) else None
                        self.evac(bank, bb, 512, stg[:], sb_, scale=scale)
                    elif fc < 16:
                        self.evac(bank, bb, 512, stg[:], sb_, scale=QSCALE)
                    elif fc < 48 or 64 <= fc < 80:
                        self.evac(bank, bb, 512, stg[:], sb_)
                    elif fc < 64:
                        ci = fc - 48
                        c0 = ci * 5
                        cw = self.cdc
                        i = rr[0] % 2
                        rr[0] += 1
                        gb, gbb = gbufs[i]
                        acc, accb = accs[i]
                        sc.op(sc.act, lambda e, gb=gb, ci=ci: e.activation(out=gb[:, 0:3], in_=halo[:, ci * 3:ci * 3 + 3], func=AF.Copy),
                              reads=[halob], writes=[gbb])
                        sc.op(sc.dve, lambda e, gb=gb, bank=bank: e.tensor_copy(out=gb[:, 3:515], in_=bank[:, 0:512]), reads=[bb], writes=[gbb])
                        sc.op(sc.act, lambda e, gb=gb, ci=ci: e.activation(out=halo[:, ci * 3:ci * 3 + 3], in_=gb[:, 512:515], func=AF.Copy),
                              reads=[gbb], writes=[halob])
                        sc.op(sc.dve, lambda e, gb=gb, acc=acc, c0=c0: e.tensor_scalar(out=acc, in0=gb[:, 3:515], scalar1=cw[:, c0 + 3:c0 + 4],
                                                                                      scalar2=cw[:, c0 + 4:c0 + 5], op0=ALU.mult, op1=ALU.add),
                              reads=[gbb, self.cdcb], writes=[accb])
                        for k in (2, 1, 0):
                            sc.op(sc.dve, lambda e, k=k, gb=gb, acc=acc, c0=c0: e.scalar_tensor_tensor(
                                out=acc, in0=gb[:, k:k + 512], scalar=cw[:, c0 + k:c0 + k + 1], in1=acc, op0=ALU.mult, op1=ALU.add),
                                reads=[gbb, self.cdcb, accb], writes=[accb])
                        if fc < 56:
                            sc.op(sc.act, lambda e, acc=acc, stg=stg: e.activation(out=stg[:], in_=acc, func=AF.Silu), reads=[accb], writes=[sb_])
                        else:
                            sc.op(sc.act, lambda e, acc=acc: e.activation(out=acc, in_=acc, func=AF.Silu), reads=[accb], writes=[accb])
                            sc.op(sc.dve, lambda e, acc=acc, stg=stg: e.tensor_scalar(out=stg[:], in0=acc, scalar1=float(QSCALE), scalar2=None, op0=ALU.mult),
                                  reads=[accb], writes=[sb_])
                    elif fc < 96:
                        self.evac(bank, bb, 512, stg[:], sb_, func=AF.Sigmoid)
                    else:
                        self.evac(bank, bb, 512, stg[:], sb_, bias=self.bif[:, 0:1], extra_reads=[self.bifb])
                    sc.op(sc.pool, lambda e, stg=stg, fc=fc, hh=hh: e.dma_start(
                        out=self.PRJ.ap[fc * 128:(fc + 1) * 128, tt * T + hh * 512: tt * T + (hh + 1) * 512], in_=stg[:]),
                        reads=[sb_], writes=[self.db("PRJ", fc)])

            self.proj([(wl, fc, "p") for fc in range(FC)], 32, rhs_fn, T, consumer)
        self.nw = 3

    def phase_attn_ab(self):
        sc = self.sc
        R1F, R1 = self.carve2("p2r", 0, 32768)
        F1 = self.carve("p2f", 34048, 2048, F32)
        QT = [R1[:, 0:2048], R1[:, 6144:8192]]
        KT = [R1[:, 2048:4096], R1[:, 8192:10240]]
        QTF = [R1F[:, 0:2048], R1F[:, 6144:8192]]
        KTF = [R1F[:, 2048:4096], R1F[:, 8192:10240]]
        VTF = [R1F[:, 4096:6144], R1F[:, 10240:12288]]
        qtb, ktb, vtb = [Buf(), Buf()], [Buf(), Buf()], [Buf(), Buf()]
        V = [R1[:, 12288:14336], R1[:, 14336:16384]]
        VF = [R1F[:, 12288:14336], R1F[:, 14336:16384]]
        vb = [Buf(), Buf()]
        Pb = [(R1[:, 16384 + i * 512: 16384 + (i + 1) * 512], R1F[:, 16384 + i * 512: 16384 + (i + 1) * 512], Buf()) for i in range(3)]
        LMA = R1[:, 17920:17920 + 8192]
        LMB = R1[:, 26112:26112 + 2048]
        lmab, lmbb = Buf(), Buf()
        biasT = [R1[:, 28160:30208], R1[:, 30208:32256]]
        biasTF = [R1F[:, 28160:30208], R1F[:, 30208:32256]]
        biasTb = [Buf(), Buf()]
        kmT = F1[:, 0:8]
        Gm = F1[:, 128:256]
        top8 = F1[:, 256:384]
        thr = F1[:, 384:400]
        bias = F1[:, 512:640]
        rden = F1[:, 1024:1536]
        kmb, gmb, t8b, thrb, biasb, rdb = Buf(), Buf(), Buf(), Buf(), Buf(), Buf()
        sc.op(sc.pool, lambda e: e.dma_start(out=R1F[:, 17920:17920 + 8192], in_=self.lma_d), writes=[lmab])
        sc.op(sc.pool, lambda e: e.dma_start(out=R1F[:, 26112:26112 + 2048], in_=self.lmb_d), writes=[lmbb])
        SB = [0, 1, 4]
        ND = [(2, 3), (5, 6)]
        AUX = 7

        def pre(hh):
            isB = hh >= 16
            h = hh % 16
            par = hh % 2
            qrow = (48 + h) * 128 if isB else h * 128
            krow = qrow + 16 * 128
            vrow = qrow + 32 * 128
            for (dstt, row, bb_) in ((QTF[par], qrow, qtb[par]), (KTF[par], krow, ktb[par]), (VTF[par], vrow, vtb[par])):
                sc.op(sc.sp, lambda e, dstt=dstt, row=row: e.dma_start(out=dstt, in_=self.PRJ.ap[row:row + 128, :]),
                      reads=[self.db("PRJ", row // 128)], writes=[bb_])
            bank, bb = self.banks[AUX], self.bankbufs[AUX]
            for g4 in range(4):
                for j in range(4):
                    kj = g4 * 4 + j
                    sc.op(sc.pe, lambda e, j=j, kj=kj: e.transpose(
                        out=bank[:, j * 128:(j + 1) * 128], in_=VTF[par][:, kj * 128:(kj + 1) * 128], identity=self.ident[:]),
                        reads=[vtb[par], self.identb], writes=[bb])
                self.evac(bank, bb, 512, VF[par][:, g4 * 512:(g4 + 1) * 512], vb[par])
            if isB:
                sc.op(sc.dve, lambda e: e.tensor_reduce(
                    out=kmT, in_=KTF[par].rearrange("p (n s) -> p n s", s=256), axis=AX.X, op=ALU.add),
                    reads=[ktb[par]], writes=[kmb])
                for i in range(16):
                    sc.op(sc.pe, lambda e, i=i: e.matmul(
                        bank[:, i * 8:(i + 1) * 8], lhsT=QTF[par][:, i * 128:(i + 1) * 128], rhs=kmT, start=True, stop=True),
                        reads=[qtb[par], kmb], writes=[bb])
                sc.op(sc.dve, lambda e: e.tensor_tensor(out=Gm, in0=bank[:, 0:128], in1=self.vm[:], op=ALU.add),
                      reads=[bb, self.vmb], writes=[gmb])
                for i in range(16):
                    sc.op(sc.dve, lambda e, i=i: e.max(out=top8[:, i * 8:(i + 1) * 8], in_=Gm[:, i * 8:(i + 1) * 8]),
                          reads=[gmb], writes=[t8b])
                sc.op(sc.dve, lambda e: e.tensor_scalar(
                    out=thr, in0=top8.rearrange("p (i k) -> p i k", k=8)[:, :, 2], scalar1=-1e29, scalar2=None, op0=ALU.max),
                    reads=[t8b], writes=[thrb])
                for i in range(16):
                    sc.op(sc.dve, lambda e, i=i: e.tensor_scalar(
                        out=bias[:, i * 8:(i + 1) * 8], in0=Gm[:, i * 8:(i + 1) * 8], scalar1=thr[:, i:i + 1], scalar2=-1.0,
                        op0=ALU.is_ge, op1=ALU.add), reads=[gmb, thrb], writes=[biasb])
                sc.op(sc.dve, lambda e: e.tensor_tensor(out=bias, in0=bias, in1=self.own[:], op=ALU.add),
                      reads=[biasb, self.ownb], writes=[biasb])
                sc.op(sc.pe, lambda e: e.transpose(out=bank[:, 128:256], in_=bias, identity=self.ident[:]),
                      reads=[biasb, self.identb], writes=[bb])
                for i in range(16):
                    sc.op(sc.dve, lambda e, i=i: e.tensor_scalar(out=biasTF[par][:, i * 128:(i + 1) * 128], in0=bank[:, 128:256],
                                                                scalar1=self.rm[:, i:i + 1], scalar2=None, op0=ALU.mult),
                          reads=[bb, self.rmb], writes=[biasTb[par]])

        steps = []
        gi = 0
        for hh in range(32):
            for g in range(4):
                for kj in range(4 * g + 4):
                    steps.append((hh, g, kj, gi))
                gi += 1

        def front(idx):
            hh, g, kj, G = steps[idx]
            isB = hh >= 16
            par = hh % 2
            sb_i = SB[idx % 3]
            sbank, sbb = self.banks[sb_i], self.bankbufs[sb_i]
            P, PF, Pbb = Pb[idx % 3]
            diag = kj >= 4 * g
            cs = max(0, kj - 4 * g) * 128
            sc.op(sc.pe, lambda e: e.matmul(
                sbank[:, cs:512], lhsT=KT[par][:, kj * 128:(kj + 1) * 128], rhs=QT[par][:, g * 512 + cs:(g + 1) * 512], start=True, stop=False),
                reads=[ktb[par], qtb[par]], writes=[sbb])
            if not isB:
                dd = 4 * g - kj + 3
                sc.op(sc.pe, lambda e: e.matmul(
                    sbank[:, cs:512], lhsT=self.identr[:], rhs=LMA[:, dd * 512 + cs:(dd + 1) * 512], start=False, stop=True),
                    reads=[self.identrb, lmab], writes=[sbb])
            else:
                n = kj // 2
                sc.op(sc.pe, lambda e: e.matmul(
                    sbank[:, cs:512], lhsT=self.en[:, n * 128:(n + 1) * 128], rhs=biasT[par][:, g * 512 + cs:(g + 1) * 512],
                    start=False, stop=(not diag)), reads=[self.enb, biasTb[par]], writes=[sbb])
                if diag:
                    jj = kj - 4 * g
                    sc.op(sc.pe, lambda e: e.matmul(
                        sbank[:, cs:512], lhsT=self.identr[:], rhs=LMB[:, jj * 512 + cs:(jj + 1) * 512], start=False, stop=True),
                        reads=[self.identrb, lmbb], writes=[sbb])
            sc.op(sc.act, lambda e: e.activation(out=PF[:, cs:512], in_=sbank[:, cs:512], func=AF.Exp), reads=[sbb], writes=[Pbb])

        def back(idx):
            hh, g, kj, G = steps[idx]
            par = hh % 2
            P, PF, Pbb = Pb[idx % 3]
            nbi, dbi = ND[G % 2]
            nbank, nbb = self.banks[nbi], self.bankbufs[nbi]
            dbank, dbb = self.banks[dbi], self.bankbufs[dbi]
            nk = 4 * g + 4
            cs = max(0, kj - 4 * g) * 128
            sc.op(sc.pe, lambda e: e.matmul(
                nbank[:, cs:512], lhsT=V[par][:, kj * 128:(kj + 1) * 128], rhs=P[:, cs:512], start=(kj == 0), stop=(kj == nk - 1)),
                reads=[vb[par], Pbb], writes=[nbb])
            sc.op(sc.pe, lambda e: e.matmul(
                dbank[:, cs:512], lhsT=self.onesr[:], rhs=P[:, cs:512], start=(kj == 0), stop=(kj == nk - 1)),
                reads=[self.onesrb, Pbb], writes=[dbb])
            if kj == nk - 1:
                sc.op(sc.dve, lambda e: e.reciprocal(out=rden, in_=dbank[:, 0:512]), reads=[dbb], writes=[rdb])
                stg, sb_ = self.rot("stage")
                sc.op(sc.dve, lambda e: e.tensor_tensor(out=stg[:], in0=nbank[:, 0:512], in1=rden, op=ALU.mult),
                      reads=[nbb, rdb], writes=[sb_])
                sc.op(sc.pool, lambda e: e.dma_start(
                    out=self.MIXT.ap[hh * 128:(hh + 1) * 128, g * 512:(g + 1) * 512], in_=stg[:]),
                    reads=[sb_], writes=[self.db("MIXT", hh)])

        LA = 2
        n = len(steps)
        pre(0)
        for idx in range(n + LA):
            if idx < n:
                hh, g, kj, G = steps[idx]
                if g == 3 and kj == 0 and hh + 1 < 32:
                    pre(hh + 1)
                front(idx)
            if idx - LA >= 0:
                back(idx - LA)

    def phase_attn_c(self):
        sc = self.sc
        R1F, R1 = self.carve2("pcr", 0, 24576)
        F1 = self.carve("pcf", 24576, 4096, F32)
        QT = [R1[:, 0:2048], R1[:, 6144:8192]]
        KT = [R1[:, 2048:4096], R1[:, 8192:10240]]
        QTF = [R1F[:, 0:2048], R1F[:, 6144:8192]]
        KTF = [R1F[:, 2048:4096], R1F[:, 8192:10240]]
        VTF = [R1F[:, 4096:6144], R1F[:, 10240:12288]]
        qtb, ktb, vtb = [Buf(), Buf()], [Buf(), Buf()], [Buf(), Buf()]
        V = [R1[:, 12288:14336], R1[:, 14336:16384]]
        VF = [R1F[:, 12288:14336], R1F[:, 14336:16384]]
        vb = [Buf(), Buf()]
        SP = [(R1[:, 16384 + i * 512:16384 + (i + 1) * 512], R1F[:, 16384 + i * 512:16384 + (i + 1) * 512], Buf()) for i in range(3)]
        AT = [(R1[:, 17920 + i * 512:17920 + (i + 1) * 512], R1F[:, 17920 + i * 512:17920 + (i + 1) * 512], Buf()) for i in range(3)]
        RACC, RACCF, raccb = R1[:, 19456:19968], R1F[:, 19456:19968], Buf()
        LMC, lmcb = R1[:, 19968:19968 + 2048], Buf()
        TRI, trigb = R1[:, 22016:22144], Buf()
        EZ = [(F1[:, i * 512:(i + 1) * 512], Buf()) for i in range(4)]
        E2 = [(F1[:, 2048 + i * 512:2048 + (i + 1) * 512], Buf()) for i in range(2)]
        sc.op(sc.pool, lambda e: e.dma_start(out=R1F[:, 19968:19968 + 2048], in_=self.lmc_d), writes=[lmcb])
        sc.op(sc.pool, lambda e: e.dma_start(out=R1F[:, 22016:22144], in_=self.trige_d), writes=[trigb])
        ZB = [0, 1, 4]
        RB = [6, 7, 5]
        OB = 2
        AUX = 3

        def pre(h):
            par = h % 2
            for (dstt, row, bb_) in ((QTF[par], h * 128, qtb[par]), (KTF[par], (16 + h) * 128, ktb[par]), (VTF[par], (32 + h) * 128, vtb[par])):
                sc.op(sc.sp, lambda e, dstt=dstt, row=row: e.dma_start(out=dstt, in_=self.PRJ.ap[row:row + 128, :]),
                      reads=[self.db("PRJ", row // 128)], writes=[bb_])
            bank, bb = self.banks[AUX], self.bankbufs[AUX]
            for g4 in range(4):
                for j in range(4):
                    kj = g4 * 4 + j
                    sc.op(sc.pe, lambda e, j=j, kj=kj: e.transpose(
                        out=bank[:, j * 128:(j + 1) * 128], in_=VTF[par][:, kj * 128:(kj + 1) * 128], identity=self.ident[:]),
                        reads=[vtb[par], self.identb], writes=[bb])
                self.evac(bank, bb, 512, VF[par][:, g4 * 512:(g4 + 1) * 512], vb[par])

        steps = []
        for h in range(16):
            for g in range(4):
                for kj in range(4 * g + 3, -1, -1):
                    steps.append((h, g, kj))

        def stF(idx):
            h, g, kj = steps[idx]
            par = h % 2
            zi = ZB[idx % 3]
            zbank, zbb = self.banks[zi], self.bankbufs[zi]
            ez, ezb = EZ[idx % 4]
            sp, spF, spb = SP[idx % 3]
            diag = kj >= 4 * g
            cs = max(0, kj - 4 * g) * 128
            sc.op(sc.pe, lambda e: e.matmul(
                zbank[:, cs:512], lhsT=KT[par][:, kj * 128:(kj + 1) * 128], rhs=QT[par][:, g * 512 + cs:(g + 1) * 512], start=True, stop=(not diag)),
                reads=[ktb[par], qtb[par]], writes=[zbb])
            if diag:
                jj = kj - 4 * g
                sc.op(sc.pe, lambda e: e.matmul(
                    zbank[:, cs:512], lhsT=self.identr[:], rhs=LMC[:, jj * 512 + cs:(jj + 1) * 512], start=False, stop=True),
                    reads=[self.identrb, lmcb], writes=[zbb])
            sc.op(sc.act, lambda e: e.activation(out=ez[:, cs:512], in_=zbank[:, cs:512], func=AF.Exp), reads=[zbb], writes=[ezb])
            sc.op(sc.act, lambda e: e.activation(out=spF[:, cs:512], in_=ez[:, cs:512], func=AF.Ln, bias=1.0), reads=[ezb], writes=[spb])

        def stM(idx):
            h, g, kj = steps[idx]
            first = (kj == 4 * g + 3)
            cs = max(0, kj - 4 * g) * 128
            ri = RB[idx % 3]
            rbank, rbb = self.banks[ri], self.bankbufs[ri]
            ez, ezb = EZ[idx % 4]
            sp, spF, spb = SP[idx % 3]
            at, atF, atb = AT[idx % 3]
            e2, e2b = E2[idx % 2]
            if first:
                sc.op(sc.dve, lambda e: e.memset(RACCF[:, 0:384], 0.0), writes=[raccb])
            sc.op(sc.pe, lambda e: e.matmul(rbank[:, cs:512], lhsT=TRI, rhs=sp[:, cs:512], start=True, stop=first), reads=[trigb, spb], writes=[rbb])
            if not first:
                sc.op(sc.pe, lambda e: e.matmul(rbank[:, cs:512], lhsT=self.onesr[:], rhs=RACC[:, cs:512], start=False, stop=True),
                      reads=[self.onesrb, raccb], writes=[rbb])
            if kj > 0:
                if first:
                    sc.op(sc.dve, lambda e: e.tensor_copy(out=RACCF[:, cs:512], in_=spF[:, cs:512]), reads=[spb], writes=[raccb])
                else:
                    sc.op(sc.dve, lambda e: e.tensor_tensor(out=RACCF[:, cs:512], in0=RACCF[:, cs:512], in1=spF[:, cs:512], op=ALU.add),
                          reads=[spb, raccb], writes=[raccb])
            sc.op(sc.act, lambda e: e.activation(out=e2[:, cs:512], in_=rbank[:, cs:512], func=AF.Exp, scale=-1.0), reads=[rbb], writes=[e2b])
            sc.op(sc.dve, lambda e: e.tensor_tensor(out=atF[:, cs:512], in0=ez[:, cs:512], in1=e2[:, cs:512], op=ALU.mult), reads=[ezb, e2b], writes=[atb])

        def stB(idx):
            h, g, kj = steps[idx]
            par = h % 2
            first = (kj == 4 * g + 3)
            cs = max(0, kj - 4 * g) * 128
            at, atF, atb = AT[idx % 3]
            obank, obb = self.banks[OB], self.bankbufs[OB]
            sc.op(sc.pe, lambda e: e.matmul(
                obank[:, cs:512], lhsT=V[par][:, kj * 128:(kj + 1) * 128], rhs=at[:, cs:512], start=first, stop=(kj == 0)),
                reads=[vb[par], atb], writes=[obb])
            if kj == 0:
                stg, sb_ = self.rot("stage")
                self.evac(obank, obb, 512, stg[:], sb_)
                sc.op(sc.pool, lambda e: e.dma_start(
                    out=self.MIXT.ap[h * 128:(h + 1) * 128, g * 512:(g + 1) * 512], in_=stg[:]),
                    reads=[sb_], writes=[self.db("MIXT", h)])

        n = len(steps)
        pre(0)
        for idx in range(n + 2):
            if idx < n:
                h, g, kj = steps[idx]
                if g == 3 and kj == 4 * g + 3 and h + 1 < 16:
                    pre(h + 1)
                stF(idx)
            if 0 <= idx - 1 < n:
                stM(idx - 1)
            if 0 <= idx - 2 < n:
                stB(idx - 2)

    def phase_mlstm(self):
        sc = self.sc
        RF, R = self.carve2("pdr", 0, 20480)
        F1 = self.carve("pdf", 20480, 12288, F32)
        QT, QTF, qtb = R[:, 0:2048], RF[:, 0:2048], Buf()
        KT, KTF, ktb = R[:, 2048:4096], RF[:, 2048:4096], Buf()
        KK, KKF, kkb = R[:, 4096:6144], RF[:, 4096:6144], Buf()
        CT, CTF, ctb = R[:, 6144:6400], RF[:, 6144:6400], Buf()
        NB, NBF, nbb = R[:, 6400:6528], RF[:, 6400:6528], Buf()
        VA3 = [(R[:, 6528 + i * 256:6528 + (i + 1) * 256], RF[:, 6528 + i * 256:6528 + (i + 1) * 256], Buf()) for i in range(3)]
        ABC3 = [(R[:, 7296 + i * 128:7296 + (i + 1) * 128], RF[:, 7296 + i * 128:7296 + (i + 1) * 128], Buf()) for i in range(3)]
        MT3 = [(R[:, 7680 + i * 128:7680 + (i + 1) * 128], RF[:, 7680 + i * 128:7680 + (i + 1) * 128], Buf()) for i in range(3)]
        VDT, vdb = F1[:, 0:4096], [Buf(), Buf()]
        ODT, odb = F1[:, 4096:8192], [Buf(), Buf()]
        VV, vvb = RF[:, 8192:12288], Buf()
        YT, ytb = RF[:, 12288:16384], [Buf(), Buf()]
        G16, g16b = F1[0:16, 8192:8192 + 2048], Buf()
        GT, gtb = F1[:, 10240:10496], Buf()
        LF, lfb = F1[:, 10496:10752], Buf()
        SM = F1[:, 10752:12288]
        LFB, lfbb = SM[:, 0:128], Buf()
        acol, acb = SM[:, 128:129], Buf()
        egc, egb = SM[:, 129:130], Buf()
        dcol, dcb = SM[:, 130:131], Buf()
        EB, ebb = SM[:, 256:384], Buf()
        DN, dnb = SM[:, 384:512], Buf()
        WW, wwb = SM[:, 512:640], Buf()
        HH, hhb = SM[:, 640:896], Buf()
        SQ, sqb = SM[:, 896:1152], Buf()
        MU, mub = SM[:, 1152:1280], Buf()
        RS, rsb = SM[:, 1280:1408], Buf()
        TM, tmb = SM[:, 1408:1536], Buf()
        EB3 = [(SM[:, 256:384], Buf()), (SM[:, 640:768], Buf()), (SM[:, 768:896], Buf())]
        EG3 = [(SM[:, 129:130], Buf()), (SM[:, 133:134], Buf()), (SM[:, 137:138], Buf())]
        LN4 = RF[:, 16384:16384 + 2560]
        SQ4 = LN4[:, 0:1024]
        MU4 = LN4[:, 1024:1536]
        RS4 = LN4[:, 1536:2048]
        TM4 = LN4[:, 2048:2560]
        sc.op(sc.sp, lambda e: e.dma_start(out=G16, in_=self.PRJ.ap[96 * 128:96 * 128 + 16, :]), reads=[self.db("PRJ", 96)], writes=[g16b])
        bank, bb = self.next_bank("aux")
        for c in range(16):
            sc.op(sc.pe, lambda e, c=c, bank=bank: e.transpose(out=bank[:, c * 16:(c + 1) * 16], in_=G16[:, c * 128:(c + 1) * 128],
                                                              identity=self.ident[0:16, 0:16]), reads=[g16b, self.identb], writes=[bb])
        sc.op(sc.dve, lambda e: e.tensor_copy(out=GT, in_=bank[:, 0:256]), reads=[bb], writes=[gtb])
        sc.op(sc.act, lambda e: e.activation(out=LF, in_=GT, func=AF.Exp, scale=-1.0), reads=[gtb], writes=[lfb])
        sc.op(sc.act, lambda e: e.activation(out=LF, in_=LF, func=AF.Ln, bias=1.0), reads=[lfb], writes=[lfb])
        sc.op(sc.dve, lambda e: e.tensor_scalar(out=LF, in0=LF, scalar1=-1.0, scalar2=None, op0=ALU.mult), reads=[lfb], writes=[lfb])
        rr = 0
        for h in range(8):
            sc.op(sc.sp, lambda e, h=h: e.dma_start(out=QTF, in_=self.PRJ.ap[(48 + h) * 128:(49 + h) * 128, :]),
                  reads=[self.db("PRJ", 48 + h)], writes=[qtb])
            sc.op(sc.sp, lambda e, h=h: e.dma_start(out=KTF, in_=self.PRJ.ap[(56 + h) * 128:(57 + h) * 128, :]),
                  reads=[self.db("PRJ", 56 + h)], writes=[ktb])
            for j in range(2):
                sc.op(sc.sp, lambda e, h=h, j=j: e.dma_start(out=VDT[:, j * 2048:(j + 1) * 2048],
                                                             in_=self.PRJ.ap[(64 + 2 * h + j) * 128:(65 + 2 * h + j) * 128, :]),
                      reads=[self.db("PRJ", 64 + 2 * h + j)], writes=[vdb[j]])
                sc.op(sc.sp, lambda e, h=h, j=j: e.dma_start(out=ODT[:, j * 2048:(j + 1) * 2048],
                                                             in_=self.PRJ.ap[(80 + 2 * h + j) * 128:(81 + 2 * h + j) * 128, :]),
                      reads=[self.db("PRJ", 80 + 2 * h + j)], writes=[odb[j]])
            for g4 in range(4):
                bank, bb = self.next_bank("aux")
                for jj in range(4):
                    c = g4 * 4 + jj
                    sc.op(sc.pe, lambda e, bank=bank, jj=jj, c=c: e.transpose(
                        out=bank[:, jj * 128:(jj + 1) * 128], in_=KTF[:, c * 128:(c + 1) * 128], identity=self.ident[:]),
                        reads=[ktb, self.identb], writes=[bb])
                self.evac(bank, bb, 512, KKF[:, g4 * 512:(g4 + 1) * 512], kkb)
            for c2 in range(8):
                bank, bb = self.next_bank("aux")
                for jj in range(4):
                    c = c2 * 2 + jj // 2
                    j = jj % 2
                    sc.op(sc.pe, lambda e, bank=bank, jj=jj, c=c, j=j: e.transpose(
                        out=bank[:, jj * 128:(jj + 1) * 128], in_=VDT[:, j * 2048 + c * 128: j * 2048 + (c + 1) * 128], identity=self.ident[:]),
                        reads=[vdb[j], self.identb], writes=[bb])
                self.evac(bank, bb, 512, VV[:, c2 * 512:(c2 + 1) * 512], vvb)
            sc.op(sc.dve, lambda e: e.memset(CTF, 0.0), writes=[ctb])
            sc.op(sc.dve, lambda e: e.memset(NBF, 0.0), writes=[nbb])

            def stA(c, h=h):
                sl = c % 3
                va, vaF, vab = VA3[sl]
                abc, abcF, abcb = ABC3[sl]
                mt, mtF, mtb = MT3[sl]
                EBs, ebbs = EB3[sl]
                egs, egbs = EG3[sl]
                lic = GT[:, c * 16 + h:c * 16 + h + 1]
                lfc = LF[:, c * 16 + 8 + h:c * 16 + 8 + h + 1]
                b1, b1b = self.banks[4], self.bankbufs[4]
                sc.op(sc.pe, lambda e: e.matmul(b1[:, 0:1], lhsT=self.tri[:], rhs=lfc, start=True, stop=True),
                      reads=[self.trib, lfb], writes=[b1b])
                sc.op(sc.dve, lambda e: e.tensor_scalar(out=LFB, in0=self.ones[:], scalar1=lfc, scalar2=None, op0=ALU.mult),
                      reads=[self.onesb, lfb], writes=[lfbb])
                sc.op(sc.pe, lambda e: e.matmul(b1[:, 128:256], lhsT=LFB, rhs=self.tri[:], start=True, stop=True),
                      reads=[lfbb, self.trib], writes=[b1b])
                sc.op(sc.dve, lambda e: e.tensor_tensor(out=dcol, in0=lic, in1=b1[:, 0:1], op=ALU.subtract),
                      reads=[gtb, b1b], writes=[dcb])
                sc.op(sc.act, lambda e: e.activation(out=acol, in_=dcol, func=AF.Exp), reads=[dcb], writes=[acb])
                sc.op(sc.act, lambda e: e.activation(out=EBs, in_=b1[:, 128:256], func=AF.Exp), reads=[b1b], writes=[ebbs])
                sc.op(sc.act, lambda e: e.activation(out=egs, in_=b1[:, 255:256], func=AF.Exp), reads=[b1b], writes=[egbs])
                sc.op(sc.dve, lambda e: e.tensor_scalar(out=vaF, in0=VV[:, c * 256:(c + 1) * 256], scalar1=acol, scalar2=None, op0=ALU.mult),
                      reads=[vvb, acb], writes=[vab])
                sc.op(sc.dve, lambda e: e.tensor_scalar(out=abcF, in0=self.ones[:], scalar1=acol, scalar2=None, op0=ALU.mult),
                      reads=[self.onesb, acb], writes=[abcb])
                qk, qkb = self.banks[5], self.bankbufs[5]
                sc.op(sc.pe, lambda e: e.matmul(qk[:, 0:128], lhsT=KT[:, c * 128:(c + 1) * 128], rhs=QT[:, c * 128:(c + 1) * 128], start=True, stop=True),
                      reads=[ktb, qtb], writes=[qkb])
                sc.op(sc.dve, lambda e: e.tensor_tensor(out=mtF, in0=qk[:, 0:128], in1=self.tri[:], op=ALU.mult),
                      reads=[qkb, self.trib], writes=[mtb])

            def stB(c, h=h):
                sl = c % 3
                va, vaF, vab = VA3[sl]
                abc, abcF, abcb = ABC3[sl]
                mt, mtF, mtb = MT3[sl]
                EBs, ebbs = EB3[sl]
                nb_, nbb_ = self.banks[6], self.bankbufs[6]
                for j in range(2):
                    sc.op(sc.pe, lambda e, j=j: e.matmul(nb_[:, j * 128:(j + 1) * 128], lhsT=va[:, j * 128:(j + 1) * 128], rhs=mt, start=True, stop=False),
                          reads=[vab, mtb], writes=[nbb_])
                    sc.op(sc.pe, lambda e, j=j: e.matmul(nb_[:, j * 128:(j + 1) * 128], lhsT=CT[:, j * 128:(j + 1) * 128], rhs=QT[:, c * 128:(c + 1) * 128], start=False, stop=True),
                          reads=[ctb, qtb], writes=[nbb_])
                sc.op(sc.pe, lambda e: e.matmul(nb_[:, 256:384], lhsT=abc, rhs=mt, start=True, stop=False),
                      reads=[abcb, mtb], writes=[nbb_])
                sc.op(sc.pe, lambda e: e.matmul(nb_[:, 256:384], lhsT=NB, rhs=QT[:, c * 128:(c + 1) * 128], start=False, stop=True),
                      reads=[nbb, qtb], writes=[nbb_])
                sc.op(sc.dve, lambda e: e.tensor_tensor(out=DN, in0=nb_[:, 256:384], in1=EBs, op=ALU.mult), reads=[nbb_, ebbs], writes=[dnb])
                sc.op(sc.act, lambda e: e.activation(out=DN, in_=DN, func=AF.Abs), reads=[dnb], writes=[dnb])
                sc.op(sc.dve, lambda e: e.tensor_scalar(out=DN, in0=DN, scalar1=1.0, scalar2=None, op0=ALU.max), reads=[dnb], writes=[dnb])
                sc.op(sc.dve, lambda e: e.reciprocal(out=DN, in_=DN), reads=[dnb], writes=[dnb])
                sc.op(sc.dve, lambda e: e.tensor_tensor(out=WW, in0=DN, in1=EBs, op=ALU.mult), reads=[dnb, ebbs], writes=[wwb])
                for j in range(2):
                    sc.op(sc.dve, lambda e, j=j: e.tensor_tensor(out=YT[:, j * 2048 + c * 128: j * 2048 + (c + 1) * 128],
                                                                in0=nb_[:, j * 128:(j + 1) * 128], in1=WW, op=ALU.mult),
                          reads=[nbb_, wwb], writes=[ytb[j]])
                if c % 4 == 3:
                    t0 = (c - 3) * 128
                    for j in range(2):
                        sc.op(sc.act, lambda e, j=j: e.activation(out=SQ4[:, j * 512:(j + 1) * 512], in_=YT[:, j * 2048 + t0: j * 2048 + t0 + 512],
                                                                 func=AF.Square), reads=[ytb[j]], writes=[sqb])
                    st_, stb_ = self.banks[7], self.bankbufs[7]
                    s2_, s2b_ = self.banks[2], self.bankbufs[2]
                    for j in range(2):
                        sc.op(sc.pe, lambda e, j=j: e.matmul(st_[:, 0:512], lhsT=self.o256[:], rhs=YT[:, j * 2048 + t0: j * 2048 + t0 + 512],
                                                            start=(j == 0), stop=(j == 1)), reads=[self.o256b, ytb[j]], writes=[stb_])
                    for j in range(2):
                        sc.op(sc.pe, lambda e, j=j: e.matmul(s2_[:, 0:512], lhsT=self.o256[:], rhs=SQ4[:, j * 512:(j + 1) * 512],
                                                            start=(j == 0), stop=(j == 1)), reads=[self.o256b, sqb], writes=[s2b_])
                    sc.op(sc.dve, lambda e: e.tensor_copy(out=MU4, in_=st_[:, 0:512]), reads=[stb_], writes=[mub])
                    sc.op(sc.dve, lambda e: e.tensor_tensor(out=TM4, in0=MU4, in1=MU4, op=ALU.mult), reads=[mub], writes=[tmb])
                    sc.op(sc.dve, lambda e: e.tensor_tensor(out=TM4, in0=s2_[:, 0:512], in1=TM4, op=ALU.subtract), reads=[s2b_, tmb], writes=[tmb])
                    sc.op(sc.dve, lambda e: e.tensor_scalar(out=TM4, in0=TM4, scalar1=float(EPS), scalar2=None, op0=ALU.add), reads=[tmb], writes=[tmb])
                    sc.op(sc.act, lambda e: e.activation(out=TM4, in_=TM4, func=AF.Ln), reads=[tmb], writes=[tmb])
                    sc.op(sc.act, lambda e: e.activation(out=RS4, in_=TM4, func=AF.Exp, scale=-0.5), reads=[tmb], writes=[rsb])
                    for j in range(2):
                        yj = YT[:, j * 2048 + t0: j * 2048 + t0 + 512]
                        sc.op(sc.dve, lambda e, yj=yj: e.tensor_tensor(out=yj, in0=yj, in1=MU4, op=ALU.subtract), reads=[ytb[j], mub], writes=[ytb[j]])
                        sc.op(sc.dve, lambda e, yj=yj: e.tensor_tensor(out=yj, in0=yj, in1=RS4, op=ALU.mult), reads=[ytb[j], rsb], writes=[ytb[j]])
                        sc.op(sc.dve, lambda e, yj=yj, j=j: e.scalar_tensor_tensor(
                            out=yj, in0=yj, scalar=self.ngc[:, 2 * h + j:2 * h + j + 1],
                            in1=ODT[:, j * 2048 + t0: j * 2048 + t0 + 512], op0=ALU.mult, op1=ALU.mult),
                            reads=[ytb[j], self.ngcb, odb[j]], writes=[ytb[j]])

            def stD(c, h=h):
                sl = c % 3
                va, vaF, vab = VA3[sl]
                abc, abcF, abcb = ABC3[sl]
                egs, egbs = EG3[sl]
                ub, ubb = self.banks[3], self.bankbufs[3]
                sc.op(sc.pe, lambda e: e.matmul(ub[:, 0:256], lhsT=KK[:, c * 128:(c + 1) * 128], rhs=va, start=True, stop=True),
                      reads=[kkb, vab], writes=[ubb])
                sc.op(sc.pe, lambda e: e.matmul(ub[:, 256:384], lhsT=KK[:, c * 128:(c + 1) * 128], rhs=abc, start=True, stop=True),
                      reads=[kkb, abcb], writes=[ubb])
                sc.op(sc.dve, lambda e: e.tensor_tensor(out=CTF, in0=ub[:, 0:256], in1=CTF, op=ALU.add), reads=[ubb, ctb], writes=[ctb])
                sc.op(sc.dve, lambda e: e.tensor_scalar(out=CTF, in0=CTF, scalar1=egs, scalar2=None, op0=ALU.mult), reads=[ctb, egbs], writes=[ctb])
                sc.op(sc.dve, lambda e: e.tensor_tensor(out=NBF, in0=ub[:, 256:384], in1=NBF, op=ALU.add), reads=[ubb, nbb], writes=[nbb])
                sc.op(sc.dve, lambda e: e.tensor_scalar(out=NBF, in0=NBF, scalar1=egs, scalar2=None, op0=ALU.mult), reads=[nbb, egbs], writes=[nbb])

            stA(0)
            stA(1)
            for c in range(16):
                stB(c)
                stD(c)
                if c + 2 < 16:
                    stA(c + 2)
            for j in range(2):
                sc.op(sc.pool, lambda e, h=h, j=j: e.dma_start(
                    out=self.MIXT.ap[(16 + 2 * h + j) * 128:(17 + 2 * h + j) * 128, :], in_=YT[:, j * 2048:(j + 1) * 2048]),
                    reads=[ytb[j]], writes=[self.db("MIXT", 16 + 2 * h + j)])

    def phase_proj_ln(self, l, j, final=False):
        sc = self.sc
        self.wqs = None
        T = 512
        if j == 0:
            parts, wl, act, res, dst = [(0, 16), (16, 16)], self.wl_out[l], self.MIXT, self.XT, self.X1T
        else:
            parts, wl, act, res, dst = [(0, 15), (15, 15), (30, 14), (44, 14), (58, 14), (72, 14)], self.wl_d[l], self.HT, self.X1T, self.XT
        NTT = S // T
        KM = max(p[1] for p in parts)
        av = [self.carve2("plA%d" % i, i * KM * T, KM * T) for i in range(2)]
        o2 = 2 * KM * T
        resF = self.carve("plR", o2, 32 * T, F32)
        abufs = [[Buf() for _ in range(KM)] for _ in range(2)]
        rbufs = [Buf() for _ in range(32)]
        gcol = self.lnp[:, (l * 2 + j) * 64:(l * 2 + j) * 64 + 32]
        bcol = self.lnp[:, (l * 2 + j) * 64 + 32:(l * 2 + j) * 64 + 64]
        if "lnt" not in self.rots:
            self.mkrot("lnt", 4, [128, 512])
        self.bank_groups["aux2"] = [6, 7]
        (mean, meanb), (m2, m2b), (rstd, rstdb), (nmr, nmrb) = [self.rots["lnt"][i] for i in range(4)]
        tmp, tmpb = nmr, nmrb
        mbank, mbb = self.banks[4], self.bankbufs[4]
        qbank, qbb = self.banks[5], self.bankbufs[5]

        def make_epi(tt):
            def epi(dc):
                z = resF[:, dc * T:(dc + 1) * T]
                stg, sb_ = self.rot("stage")
                sc.op(sc.dve, lambda e: e.tensor_tensor(out=z, in0=z, in1=rstd[:, 0:T], op=ALU.mult),
                      reads=[rbufs[dc], rstdb], writes=[rbufs[dc]])
                sc.op(sc.dve, lambda e: e.tensor_tensor(out=z, in0=z, in1=nmr[:, 0:T], op=ALU.add),
                      reads=[rbufs[dc], nmrb], writes=[rbufs[dc]])
                sc.op(sc.dve, lambda e: e.tensor_scalar(
                    out=stg[:, 0:T], in0=z, scalar1=gcol[:, dc:dc + 1], scalar2=bcol[:, dc:dc + 1], op0=ALU.mult, op1=ALU.add),
                    reads=[rbufs[dc], self.lnpb], writes=[sb_])
                if not final:
                    sc.op(sc.pool, lambda e: e.dma_start(
                        out=dst.ap[dc * 128:(dc + 1) * 128, tt * T:(tt + 1) * T], in_=stg[:, 0:T]),
                        reads=[sb_], writes=[self.db(dst.name, dc)])
                else:
                    bank, bb = self.next_bank("aux2")
                    nts = T // 128
                    for ts in range(nts):
                        sc.op(sc.pe, lambda e, ts=ts: e.transpose(
                            out=bank[:, ts * 128:(ts + 1) * 128], in_=stg[:, ts * 128:(ts + 1) * 128], identity=self.ident[:]),
                            reads=[sb_, self.identb], writes=[bb])
                    o2s, o2b = self.rot("stage")
                    self.evac(bank, bb, T, o2s[:, 0:T], o2b)
                    ov = self.out[tt * T:(tt + 1) * T, dc * 128:(dc + 1) * 128].rearrange("(s p) d -> p s d", p=128)
                    sc.op(sc.pool, lambda e: e.dma_start(
                        out=ov, in_=o2s[:, 0:T].rearrange("p (s d) -> p s d", d=128)),
                        reads=[o2b], writes=[self.db("out", dc)])
            return epi

        pending = [None]
        issued = [0]
        pc = 0
        for tt in range(NTT):
            issued[0] = 0
            for pi, (kb, kn) in enumerate(parts):
                bi = pc % 2
                pc += 1
                actvF, actv = av[bi]
                self.load_actT(act, kb * 128, kn, tt * T, T, actvF, abufs[bi])
                rhs_fn = lambda kc, hh, actv=actv, bi=bi: (actv[:, kc * T:(kc + 1) * T], abufs[bi][kc])
                firstp = (pi == 0)
                lastp = (pi == len(parts) - 1)

                def consumer(tag, dc, bks, firstp=firstp, lastp=lastp, tt=tt):
                    bank, bb = bks[0]
                    z = resF[:, dc * T:(dc + 1) * T]
                    if firstp:
                        while issued[0] <= min(dc + 2, 31):
                            k = issued[0]
                            issued[0] += 1
                            if pending[0] is not None:
                                pending[0](k)
                            sv = res.ap[k * 128:(k + 1) * 128, tt * T:(tt + 1) * T]
                            zk = resF[:, k * T:(k + 1) * T]
                            sc.op(sc.pool, lambda e, zk=zk, sv=sv: e.dma_start(out=zk, in_=sv), reads=[self.db(res.name, k)], writes=[rbufs[k]])
                        sc.op(sc.dve, lambda e: e.scalar_tensor_tensor(out=z, in0=z, scalar=float(ALPHA), in1=bank[:, 0:T],
                                                                        op0=ALU.mult, op1=ALU.add),
                              reads=[bb, rbufs[dc]], writes=[rbufs[dc]])
                    else:
                        sc.op(sc.dve, lambda e: e.tensor_tensor(out=z, in0=z, in1=bank[:, 0:T], op=ALU.add),
                              reads=[bb, rbufs[dc]], writes=[rbufs[dc]])
                    if lastp:
                        sq, sqb = self.rot("stage")
                        sc.op(sc.act, lambda e: e.activation(out=sq[:, 0:T], in_=z, func=AF.Square), reads=[rbufs[dc]], writes=[sqb])
                        sc.op(sc.pe, lambda e: e.matmul(mbank[:, 0:T], lhsT=self.onesD[:], rhs=z, start=(dc == 0), stop=(dc == 31)),
                              reads=[self.onesDb, rbufs[dc]], writes=[mbb])
                        sc.op(sc.pe, lambda e: e.matmul(qbank[:, 0:T], lhsT=self.onesD[:], rhs=sq[:, 0:T], start=(dc == 0), stop=(dc == 31)),
                              reads=[self.onesDb, sqb], writes=[qbb])

                self.proj([(wl, dc, "p") for dc in range(32)], kn, rhs_fn, T, consumer, kbase=kb)
            sc.op(sc.dve, lambda e: e.tensor_copy(out=mean[:, 0:T], in_=mbank[:, 0:T]), reads=[mbb], writes=[meanb])
            sc.op(sc.dve, lambda e: e.tensor_tensor(out=m2[:, 0:T], in0=mean[:, 0:T], in1=mean[:, 0:T], op=ALU.mult),
                  reads=[meanb], writes=[m2b])
            sc.op(sc.dve, lambda e: e.tensor_tensor(out=m2[:, 0:T], in0=qbank[:, 0:T], in1=m2[:, 0:T], op=ALU.subtract),
                  reads=[qbb, m2b], writes=[m2b])
            sc.op(sc.dve, lambda e: e.tensor_scalar(out=m2[:, 0:T], in0=m2[:, 0:T], scalar1=float(EPS), scalar2=None, op0=ALU.add),
                  reads=[m2b], writes=[m2b])
            sc.op(sc.act, lambda e: e.activation(out=tmp[:, 0:T], in_=m2[:, 0:T], func=AF.Ln), reads=[m2b], writes=[tmpb])
            sc.op(sc.act, lambda e: e.activation(out=rstd[:, 0:T], in_=tmp[:, 0:T], func=AF.Exp, scale=-0.5),
                  reads=[tmpb], writes=[rstdb])
            sc.op(sc.dve, lambda e: e.scalar_tensor_tensor(out=nmr[:, 0:T], in0=mean[:, 0:T], scalar=-1.0, in1=rstd[:, 0:T],
                                                            op0=ALU.mult, op1=ALU.mult), reads=[meanb, rstdb], writes=[nmrb])
            pending[0] = make_epi(tt)
        for dc in range(32):
            pending[0](dc)

    def phase_ffn_upgate(self, l):
        sc = self.sc
        self.wqs = None
        self.nw = 2
        T = 1024
        abufs = [Buf() for _ in range(32)]
        X1F, X1 = self.carve2("ffx", 0, 32768)
        AFv = self.carve("fff", 42240, 4096, F32)
        halo = AFv[:, 0:2 * NFF]
        halob = Buf()
        gbufs = [(AFv[:, 256 + i * 520: 256 + i * 520 + 514], Buf()) for i in range(2)]
        accs = [(AFv[:, 1400 + i * 512: 1400 + (i + 1) * 512], Buf()) for i in range(2)]
        sil = [(AFv[:, 2500 + i * 512: 2500 + (i + 1) * 512], Buf()) for i in range(2)]
        sc.op(sc.dve, lambda e: e.memset(halo, 0.0), writes=[halob])
        cw = self.ffc
        rr = [0]
        state = {}
        for tt in range(S // T):
            self.load_actT(self.X1T, 0, 32, tt * T, T, X1F, abufs)
            rhs_fn = lambda kc, hh: (X1[:, kc * T + hh * 512: kc * T + (hh + 1) * 512], abufs[kc])

            def consumer(tag, fc, bks, tt=tt):
                c0 = (l * NFF + fc) * 4
                if tag == "g":
                    state["cur"] = []
                    for hh, (bank, bb) in enumerate(bks):
                        i = rr[0] % 2
                        rr[0] += 1
                        gb, gbb = gbufs[i]
                        acc, accb = accs[i]
                        sl, slb = sil[i]
                        state["cur"].append((sl, slb))
                        sc.op(sc.act, lambda e, gb=gb: e.activation(out=gb[:, 0:2], in_=halo[:, fc * 2:fc * 2 + 2], func=AF.Copy),
                              reads=[halob], writes=[gbb])
                        sc.op(sc.dve, lambda e, gb=gb, bank=bank: e.tensor_copy(out=gb[:, 2:514], in_=bank[:, 0:512]), reads=[bb], writes=[gbb])
                        sc.op(sc.act, lambda e, gb=gb: e.activation(out=halo[:, fc * 2:fc * 2 + 2], in_=gb[:, 512:514], func=AF.Copy),
                              reads=[gbb], writes=[halob])
                        sc.op(sc.dve, lambda e, gb=gb, acc=acc: e.tensor_scalar(out=acc, in0=gb[:, 2:514], scalar1=cw[:, c0 + 2:c0 + 3],
                                                                              scalar2=cw[:, c0 + 3:c0 + 4], op0=ALU.mult, op1=ALU.add),
                              reads=[gbb, self.ffcb], writes=[accb])
                        sc.op(sc.dve, lambda e, gb=gb, acc=acc: e.scalar_tensor_tensor(out=acc, in0=gb[:, 1:513], scalar=cw[:, c0 + 1:c0 + 2], in1=acc,
                                                                                      op0=ALU.mult, op1=ALU.add),
                              reads=[gbb, self.ffcb, accb], writes=[accb])
                        sc.op(sc.dve, lambda e, gb=gb, acc=acc: e.scalar_tensor_tensor(out=acc, in0=gb[:, 0:512], scalar=cw[:, c0:c0 + 1], in1=acc,
                                                                                      op0=ALU.mult, op1=ALU.add),
                              reads=[gbb, self.ffcb, accb], writes=[accb])
                        sc.op(sc.act, lambda e, sl=sl, acc=acc: e.activation(out=sl, in_=acc, func=AF.Silu), reads=[accb], writes=[slb])
                else:
                    for hh, (bank, bb) in enumerate(bks):
                        sl, slb = state["cur"][hh]
                        stg, sb_ = self.rot("stage")
                        sc.op(sc.dve, lambda e, stg=stg, bank=bank, sl=sl: e.tensor_tensor(out=stg[:], in0=bank[:, 0:512], in1=sl, op=ALU.mult),
                              reads=[bb, slb], writes=[sb_])
                        sc.op(sc.pool, lambda e, stg=stg, hh=hh: e.dma_start(
                            out=self.HT.ap[fc * 128:(fc + 1) * 128, tt * T + hh * 512: tt * T + (hh + 1) * 512], in_=stg[:]),
                            reads=[sb_], writes=[self.db("HT", fc)])

            plan = []
            for fc in range(NFF):
                plan.append((self.wl_g[l], fc, "g"))
                plan.append((self.wl_u[l], fc, "u"))
            self.proj(plan, 32, rhs_fn, T, consumer)
        self.nw = 3


def _wl(W):
    K, F = W.shape
    KC, FC = K // 128, F // 128
    return np.ascontiguousarray(W.reshape(KC, 128, FC, 128).transpose(2, 1, 0, 3)).reshape(FC, 128, KC * 128)


def _consts():
    c = {}
    c["ident"] = np.eye(128, dtype=np.float32)
    c["identr"] = np.eye(128, dtype=np.float32)
    c["onesr"] = np.ones((128, 128), np.float32)
    c["onesD"] = np.full((128, 128), 1.0 / D, np.float32)
    sl = np.arange(128)[:, None]
    tl = np.arange(512)[None, :]
    lma = np.zeros((128, 16, 512), np.float32)
    for dd in range(16):
        dist = 128 * (dd - 3) + tl - sl
        cnt = ((dist >= 0) & (dist <= 128)).astype(np.float64) \
            + ((dist >= 0) & (dist % 4 == 0) & (dist <= 512)) + ((dist >= 0) & (dist % 16 == 0) & (dist <= 2048))
        lma[:, dd, :] = np.where(cnt > 0, np.log(np.maximum(cnt, 1.0)), NEG)
    c["lma"] = lma.reshape(128, 16 * 512)
    lmb = np.zeros((128, 4, 512), np.float32)
    for j in range(4):
        lmb[:, j, :] = np.where(sl + 128 * j <= tl, 0.0, NEG)
    c["lmb"] = lmb.reshape(128, 4 * 512)
    en = np.zeros((16, 8, 8, 128), np.float32)
    for n in range(8):
        en[:, n, n, :] = -NEG
    c["en"] = en.reshape(128, 1024)
    rm = np.zeros((16, 8, 16), np.float32)
    for i in range(16):
        rm[i, :, i] = 1.0
    c["rm"] = rm.reshape(128, 16)
    vm = np.zeros((128, 16, 8), np.float32)
    own = np.zeros((128, 16, 8), np.float32)
    for i in range(16):
        for n in range(8):
            if n >= i // 2:
                vm[:, i, n] = -1e30
            if n == i // 2:
                own[:, i, n] = 1.0
    lmc = np.zeros((128, 4, 512), np.float32)
    for j in range(4):
        lmc[:, j, :] = np.where(sl + 128 * j < tl, 0.0, NEG)
    c["lmc"] = lmc.reshape(128, 4 * 512)
    jj = np.arange(128)
    c["tri"] = (jj[:, None] <= jj[None, :]).astype(np.float32)
    c["trige"] = (jj[:, None] >= jj[None, :]).astype(np.float32)
    c["ones"] = np.ones((128, 128), np.float32)
    c["o256"] = np.full((128, 128), 1.0 / 256, np.float32)
    c["vm"] = vm.reshape(128, 128)
    c["own"] = own.reshape(128, 128)
    return c


def _prep_shared(inp):
    sh = _consts()
    sh["wl_in0"] = _wl(inp["w_in_ab"][0])
    sh["wl_out0"] = _wl(inp["w_out_ab"][0])
    wcd = np.zeros((D, 97 * 128), np.float32)
    wcd[:, :12304] = inp["w_in_cd"][0]
    sh["wl_in1"] = _wl(wcd)
    sh["wl_out1"] = _wl(inp["w_out_cd"][0])
    for l in range(2):
        sh["wl_g%d" % l] = _wl(inp["ffn_w_gate"][l])
        sh["wl_u%d" % l] = _wl(inp["ffn_w_up"][l])
        sh["wl_d%d" % l] = _wl(inp["ffn_w_down"][l])
    lnp = np.zeros((128, 256), np.float32)
    for l in range(2):
        for j in range(2):
            o = (l * 2 + j) * 64
            lnp[:, o:o + 32] = inp["ln_g"][l, j].reshape(32, 128).T
            lnp[:, o + 32:o + 64] = inp["ln_b"][l, j].reshape(32, 128).T
    sh["lnp"] = lnp
    ffc = np.zeros((128, 2, NFF, 4), np.float32)
    for l in range(2):
        for k in range(3):
            ffc[:, l, :, k] = inp["ffn_conv"][l, k].reshape(NFF, 128).T
        ffc[:, l, :, 3] = inp["ffn_conv_b"][l].reshape(NFF, 128).T
    sh["ffc"] = ffc.reshape(128, 2 * NFF * 4)
    cdc = np.zeros((128, 16, 5), np.float32)
    for k in range(4):
        cdc[:, :, k] = inp["conv_cd"][0, k].reshape(16, 128).T
    cdc[:, :, 4] = inp["conv_cd_b"][0].reshape(16, 128).T
    sh["cdc"] = cdc.reshape(128, 80)
    sh["ngc"] = np.ascontiguousarray(inp["norm_cd_g"][0].reshape(16, 128).T)
    bif = np.zeros((128, 1), np.float32)
    bif[:16, 0] = inp["b_if_cd"][0]
    sh["bif"] = bif
    return sh


def run(inp, cores=8, dump=(), stop_after=None, trace=False):
    b = Builder(dump=dump, stop_after=stop_after)
    nc = b.build()
    sh = _prep_shared(inp)
    in_maps = []
    for c in range(cores):
        m = {k: sh[k] for k in b.in_names if k in sh}
        m["x"] = np.ascontiguousarray(inp["x"][c])
        in_maps.append(m)
    res = run_bass_kernel_spmd(nc, in_maps, core_ids=list(range(cores)), trace=trace)
    return res


def kernel(**inputs):
    inp = {k: np.asarray(v) for k, v in inputs.items()}
    res = run(inp, cores=8)
    return np.stack([r["out"] for r in res.results], axis=0)
```
